# Optimizing a Trainium2 kernel written in Bass

```python
import jax, jax.numpy as jnp
from jax import lax
import numpy as np

D_MODEL = 1024
BATCH = 8
SEQ = 2048
DEPTH = 2
DEC_BATCH = 128
DEC_SEQ = 1
PAST_LEN = 2048
PAGE_SIZE = 128

N_HEADS = 8
HEAD_DIM = D_MODEL // N_HEADS
ROT_DIM = HEAD_DIM // 4
ROPE_THETA = 500000.0
MOBA_BLOCK = 256
MOBA_TOP_K = 3
Q_CHUNK = 16
CONV_W = 3
N_CONV_LAYERS = (DEPTH + 1) // 2
N_ATTN_LAYERS = DEPTH // 2
DN_ALPHA = (2.0 * DEPTH) ** 0.25
DN_BETA = (8.0 * DEPTH) ** -0.25
LN_EPS = 1e-5

kernel_name = 'hybrid_shortconv_moba_deepnorm_step'


def _layernorm(x, g, b):
    xf = x.astype(jnp.float32)
    mu = xf.mean(-1, keepdims=True)
    var = jnp.square(xf - mu).mean(-1, keepdims=True)
    return ((xf - mu) * lax.rsqrt(var + LN_EPS) * g + b).astype(x.dtype)


def _rope(x, pos):
    half = ROT_DIM // 2
    inv = ROPE_THETA ** (-jnp.arange(half, dtype=jnp.float32) * 2.0 / ROT_DIM)
    ang = pos.astype(jnp.float32)[:, None] * inv[None, :]
    cos = jnp.cos(ang)[None, :, None, :]
    sin = jnp.sin(ang)[None, :, None, :]
    xr = x[..., :ROT_DIM].astype(jnp.float32)
    x1, x2 = xr[..., :half], xr[..., half:]
    rot = jnp.concatenate([x1 * cos - x2 * sin, x2 * cos + x1 * sin], axis=-1).astype(x.dtype)
    return jnp.concatenate([rot, x[..., ROT_DIM:]], axis=-1)


def _short_conv_mixer(x, prev, w_in, w_conv, w_out):
    T = x.shape[1]
    b_gate, c_gate, h, z = jnp.split(x @ w_in, 4, axis=-1)
    u = c_gate * h
    up = jnp.concatenate([prev.astype(u.dtype), u], axis=1)
    conv = sum(w_conv[j] * up[:, j:j + T] for j in range(CONV_W))
    y = (b_gate * conv * jax.nn.silu(z)) @ w_out
    return y, up[:, -(CONV_W - 1):]


def _attn_inputs(x, pos, w_in):
    B, T, _ = x.shape
    q, k, v, z = jnp.split(x @ w_in, 4, axis=-1)
    heads = lambda t: t.reshape(B, T, N_HEADS, HEAD_DIM)
    return _rope(heads(q), pos), _rope(heads(k), pos), heads(v), z


def _moba_core(q, q_pos, kmean, gather_sel, k_own, v_own, own_pos):
    B, Q, H, _ = q.shape
    scale = HEAD_DIM ** -0.5
    s_own = jnp.einsum('bqhd,bkhd->bhqk', q, k_own, preferred_element_type=jnp.float32) * scale
    own_ok = own_pos[None, :] <= q_pos[:, None]
    s_own = jnp.where(own_ok[None, None], s_own, -jnp.inf)
    n_cand = kmean.shape[1]
    k_sel = min(MOBA_TOP_K, n_cand)
    if k_sel == 0:
        p = jax.nn.softmax(s_own, axis=-1).astype(v_own.dtype)
        return jnp.einsum('bhqk,bkhd->bqhd', p, v_own)
    qb = (q_pos // MOBA_BLOCK)[None, None, :, None]
    gate = jnp.einsum('bqhd,bnhd->bhqn', q, kmean, preferred_element_type=jnp.float32)
    gate = jnp.where(jnp.arange(n_cand)[None, None, None, :] < qb, gate, -jnp.inf)
    _, idx = lax.top_k(gate, k_sel)
    sel_ok = jnp.repeat(idx < qb, MOBA_BLOCK, axis=-1)
    pos = (idx[..., None] * MOBA_BLOCK + jnp.arange(MOBA_BLOCK)).reshape(B, H, Q, k_sel * MOBA_BLOCK)
    ks, vs = gather_sel(pos)
    s_sel = jnp.einsum('bqhd,bhqld->bhql', q, ks, preferred_element_type=jnp.float32) * scale
    s_sel = jnp.where(sel_ok, s_sel, -jnp.inf)
    p = jax.nn.softmax(jnp.concatenate([s_sel, s_own], axis=-1), axis=-1).astype(vs.dtype)
    n_sel = pos.shape[-1]
    return (jnp.einsum('bhql,bhqld->bqhd', p[..., :n_sel], vs)
            + jnp.einsum('bhqk,bkhd->bqhd', p[..., n_sel:], v_own))


def _moba_prompt(q, k, v):
    B, T, H, hd = q.shape
    nb = -(-T // MOBA_BLOCK)
    padn = nb * MOBA_BLOCK - T
    kpad = jnp.pad(k, ((0, 0), (0, padn), (0, 0), (0, 0)))
    vpad = jnp.pad(v, ((0, 0), (0, padn), (0, 0), (0, 0)))
    kmean = kpad.astype(jnp.float32).reshape(B, nb, MOBA_BLOCK, H, hd).mean(2).astype(q.dtype)
    bidx = jnp.arange(B)[:, None, None, None]
    hidx = jnp.arange(H)[None, :, None, None]

    def gather_sel(pos):
        return kpad[bidx, pos, hidx], vpad[bidx, pos, hidx]

    def chunk(args):
        qc, pc = args
        start = (pc[0] // MOBA_BLOCK) * MOBA_BLOCK
        k_own = lax.dynamic_slice_in_dim(kpad, start, MOBA_BLOCK, axis=1)
        v_own = lax.dynamic_slice_in_dim(vpad, start, MOBA_BLOCK, axis=1)
        own_pos = start + jnp.arange(MOBA_BLOCK, dtype=jnp.int32)
        return _moba_core(qc, pc, kmean, gather_sel, k_own, v_own, own_pos)

    n_ch = T // Q_CHUNK
    qs = q.reshape(B, n_ch, Q_CHUNK, H, hd).transpose(1, 0, 2, 3, 4)
    ps = jnp.arange(T, dtype=jnp.int32).reshape(n_ch, Q_CHUNK)
    out = lax.map(chunk, (qs, ps))
    return out.transpose(1, 0, 2, 3, 4).reshape(B, T, H, hd)


def _moba_sample(q, k_new, v_new, cache_k, cache_v, layer, page_table):
    B, Q, H, hd = q.shape
    ppb = MOBA_BLOCK // PAGE_SIZE
    n_full = PAST_LEN // MOBA_BLOCK
    own_start = n_full * MOBA_BLOCK
    if n_full > 0:
        rows = cache_k[layer, page_table[:, :n_full * ppb]]
        kmean = rows.astype(jnp.float32).reshape(B, n_full, MOBA_BLOCK, H, hd).mean(2).astype(q.dtype)
    else:
        kmean = jnp.zeros((B, 0, H, hd), q.dtype)
    own_pages = page_table[:, own_start // PAGE_SIZE:]
    n_own_past = PAST_LEN - own_start
    k_own = jnp.concatenate([cache_k[layer, own_pages].reshape(B, n_own_past, H, hd).astype(k_new.dtype), k_new], axis=1)
    v_own = jnp.concatenate([cache_v[layer, own_pages].reshape(B, n_own_past, H, hd).astype(v_new.dtype), v_new], axis=1)
    own_pos = own_start + jnp.arange(n_own_past + Q, dtype=jnp.int32)
    q_pos = PAST_LEN + jnp.arange(Q, dtype=jnp.int32)
    bidx = jnp.arange(B)[:, None, None, None]
    hidx = jnp.arange(H)[None, :, None, None]

    def gather_sel(pos):
        phys = page_table[bidx, pos // PAGE_SIZE]
        off = pos % PAGE_SIZE
        return cache_k[layer, phys, off, hidx], cache_v[layer, phys, off, hidx]

    return _moba_core(q, q_pos, kmean, gather_sel, k_own, v_own, own_pos)


def setup_inputs(seed: int = 0) -> dict:
    key = jax.random.key(seed)
    ks = jax.random.split(key, 13)
    n_pages = PAST_LEN // PAGE_SIZE
    n_used = DEC_BATCH * n_pages
    n_phys = n_used + max(1, n_used // 4)
    d = D_MODEL
    f32 = jnp.float32
    s_in = d ** -0.5
    s_out = d ** -0.5 * DN_BETA
    x_prompt = jax.random.normal(ks[0], (BATCH, SEQ, d), f32)
    x_sample = jax.random.normal(ks[1], (DEC_BATCH, DEC_SEQ, d), f32)
    state_conv = jax.random.normal(ks[2], (N_CONV_LAYERS, DEC_BATCH, CONV_W - 1, d), f32)
    cache_k = jax.random.normal(ks[3], (N_ATTN_LAYERS, n_phys, PAGE_SIZE, N_HEADS, HEAD_DIM), f32)
    cache_v = jax.random.normal(ks[4], (N_ATTN_LAYERS, n_phys, PAGE_SIZE, N_HEADS, HEAD_DIM), f32)
    page_table = jax.random.permutation(ks[5], n_phys)[:n_used].reshape(DEC_BATCH, n_pages).astype(jnp.int32)
    w_in_conv = jax.random.normal(ks[6], (N_CONV_LAYERS, d, 4 * d), f32) * s_in
    w_conv = jax.random.normal(ks[7], (N_CONV_LAYERS, CONV_W, d), f32) * CONV_W ** -0.5
    w_out_conv = jax.random.normal(ks[8], (N_CONV_LAYERS, d, d), f32) * s_out
    w_in_attn = jax.random.normal(ks[9], (N_ATTN_LAYERS, d, 4 * d), f32) * s_in
    w_out_attn = jax.random.normal(ks[10], (N_ATTN_LAYERS, d, d), f32) * s_out
    ln_g = 1.0 + 0.02 * jax.random.normal(ks[11], (DEPTH, d), f32)
    ln_b = 0.02 * jax.random.normal(ks[12], (DEPTH, d), f32)
    return {'x_prompt': x_prompt, 'x_sample': x_sample, 'state_conv': state_conv,
            'cache_k': cache_k, 'cache_v': cache_v, 'page_table': page_table,
            'w_in_conv': w_in_conv, 'w_conv': w_conv, 'w_out_conv': w_out_conv,
            'w_in_attn': w_in_attn, 'w_out_attn': w_out_attn, 'ln_g': ln_g, 'ln_b': ln_b}


def reference(x_prompt, x_sample, state_conv, cache_k, cache_v, page_table,
              w_in_conv, w_conv, w_out_conv, w_in_attn, w_out_attn, ln_g, ln_b):
    pos_p = jnp.arange(x_prompt.shape[1], dtype=jnp.int32)
    pos_s = PAST_LEN + jnp.arange(x_sample.shape[1], dtype=jnp.int32)
    xp, xs = x_prompt, x_sample
    conv_p, conv_s, k_p, v_p, k_s, v_s = [], [], [], [], [], []
    for i in range(DEPTH):
        j = i // 2
        if i % 2 == 0:
            zeros = jnp.zeros((xp.shape[0], CONV_W - 1, D_MODEL), xp.dtype)
            yp, sp = _short_conv_mixer(xp, zeros, w_in_conv[j], w_conv[j], w_out_conv[j])
            ys, ss = _short_conv_mixer(xs, state_conv[j], w_in_conv[j], w_conv[j], w_out_conv[j])
            conv_p.append(sp)
            conv_s.append(ss)
        else:
            Bp, Tp, _ = xp.shape
            q, k, v, z = _attn_inputs(xp, pos_p, w_in_attn[j])
            o = _moba_prompt(q, k, v)
            yp = (o.reshape(Bp, Tp, D_MODEL) * jax.nn.silu(z)) @ w_out_attn[j]
            k_p.append(k.reshape(Bp, Tp // PAGE_SIZE, PAGE_SIZE, N_HEADS, HEAD_DIM))
            v_p.append(v.reshape(Bp, Tp // PAGE_SIZE, PAGE_SIZE, N_HEADS, HEAD_DIM))
            Bs, Ts, _ = xs.shape
            qs_, ks_, vs_, zs = _attn_inputs(xs, pos_s, w_in_attn[j])
            os_ = _moba_sample(qs_, ks_, vs_, cache_k, cache_v, j, page_table)
            ys = (os_.reshape(Bs, Ts, D_MODEL) * jax.nn.silu(zs)) @ w_out_attn[j]
            k_s.append(ks_)
            v_s.append(vs_)
        xp = _layernorm(DN_ALPHA * xp + yp, ln_g[i], ln_b[i])
        xs = _layernorm(DN_ALPHA * xs + ys, ln_g[i], ln_b[i])
    return (xp, xs, jnp.stack(conv_p), jnp.stack(conv_s), jnp.stack(k_p), jnp.stack(v_p), jnp.stack(k_s), jnp.stack(v_s))
```

```python
import contextlib
import numpy as np
import concourse.bass as bass
import concourse.mybir as mybir
from concourse.bass_utils import run_bass_kernel_spmd

F32 = mybir.dt.float32
BF16 = mybir.dt.bfloat16
I32 = mybir.dt.int32
AF = mybir.ActivationFunctionType
ALU = mybir.AluOpType
AX = mybir.AxisListType

D = 1024
T = 2048
NSA = 128
NS = 16
NPG = 16
NPHYS = 2560
HD = 128
NH = 8
NQ = 4
CH = 128 * HD // NQ
TPC = 128 // NQ
ALPHA = float((2.0 * 2) ** 0.25)
EPS = 1e-5
SCALE = float(HD ** -0.5)
NEG = -30000.0
BIG = 1.0e30


class Op:
    __slots__ = ("e", "fn", "deps", "dma", "sig", "cnt", "sem")

    def __init__(self, e, fn, deps, dma):
        self.e, self.fn, self.deps, self.dma = e, fn, deps, dma
        self.sig = False
        self.cnt = None
        self.sem = None


class Res:
    __slots__ = ("w", "r")

    def __init__(self):
        self.w = []
        self.r = []


def _push(lst, o):
    if not o.dma:
        for i, x in enumerate(lst):
            if (not x.dma) and x.e == o.e:
                lst[i] = o
                return
    lst.append(o)


class Prog:
    def __init__(self, nc, n_dma_sems=12):
        self.nc = nc
        self.engs = {"pe": nc.tensor, "act": nc.scalar, "dve": nc.vector, "pool": nc.gpsimd, "sp": nc.sync}
        self.ops = {e: [] for e in self.engs}
        self.n_dma_sems = n_dma_sems
        self.R = {}

    def res(self, name):
        if name not in self.R:
            self.R[name] = Res()
        return self.R[name]

    def _mk(self, e, fn, reads, writes, deps, dma):
        reads = [self.res(r) if isinstance(r, str) else r for r in reads]
        writes = [self.res(w) if isinstance(w, str) else w for w in writes]
        dl = [d for d in deps if d is not None]
        for r in reads:
            dl.extend(r.w)
        for w in writes:
            dl.extend(w.w)
            dl.extend(w.r)
        if e == "pe":
            dl = [d for d in dl if d.e != "pe" or d.dma]
        o = Op(e, fn, dl, dma)
        for r in reads:
            _push(r.r, o)
        for w in writes:
            w.w = [o]
            w.r = []
        self.ops[e].append(o)
        return o

    def op(self, e, fn, reads=(), writes=(), deps=()):
        return self._mk(e, fn, reads, writes, deps, False)

    def dma(self, e, fn, reads=(), writes=(), deps=()):
        return self._mk(e, fn, reads, writes, deps, True)

    def emit(self):
        nc = self.nc
        for e, lst in self.ops.items():
            for o in lst:
                for d in o.deps:
                    d.sig = True
        with contextlib.ExitStack() as st:
            csem = {e: st.enter_context(nc.semaphore("c_" + e)) for e in self.engs}
            dsem = {e: [st.enter_context(nc.semaphore("d_%s%d" % (e, i))) for i in range(self.n_dma_sems)]
                    for e in ("sp", "act", "pool")}
            for e, lst in self.ops.items():
                c = 0
                dcnt = [0] * self.n_dma_sems
                prev = [None] * self.n_dma_sems
                k = 0
                for o in lst:
                    if o.dma:
                        j = k % self.n_dma_sems
                        k += 1
                        dcnt[j] += 16
                        o.sem, o.cnt = dsem[e][j], dcnt[j]
                        if prev[j] is not None:
                            o.deps.append(prev[j])
                        prev[j] = o
                    elif o.sig:
                        c += 1
                        o.sem, o.cnt = csem[e], c
            block = st.enter_context(nc.Block())

            def mk(e):
                def body(eng):
                    waited = {}
                    for o in self.ops[e]:
                        for d in o.deps:
                            key = id(d.sem)
                            if waited.get(key, 0) >= d.cnt:
                                continue
                            eng.wait_ge(d.sem, d.cnt)
                            waited[key] = d.cnt
                        ins = o.fn(eng)
                        if o.dma:
                            ins.then_inc(o.sem, 16)
                        elif o.sig:
                            ins.then_inc(o.sem, 1)
                    last = {}
                    for o in self.ops[e]:
                        if o.dma:
                            last[id(o.sem)] = o
                    for o in last.values():
                        if waited.get(id(o.sem), 0) < o.cnt:
                            eng.wait_ge(o.sem, o.cnt)
                return body

            block.tensor(mk("pe"))
            block.scalar(mk("act"))
            block.vector(mk("dve"))
            block.gpsimd(mk("pool"))
            block.sync(mk("sp"))


class Common:
    def __init__(self, nc, P, sb, ps, nb=2):
        self.nc, self.P, self.sb, self.ps = nc, P, sb, ps
        self.ident = sb("ident", [128, 128], F32)
        self.identb = sb("identb", [128, 128], BF16)
        self.wbb = [sb("wbb%d" % i, [128, 8, 512], BF16) for i in range(2)]
        self.wob = sb("wob", [128, 8, D], BF16)
        self.gbt = sb("gbt", [128, D], F32)
        self.bbt = sb("bbt", [128, D], F32)
        NB = nb
        self.NB = NB
        self.xt_ring = [sb("xt%d" % i, [128, D], F32) for i in range(NB)]
        self.x1t = [sb("x1t%d" % i, [128, D], F32) for i in range(NB)]
        self.stats_l = [sb("stats%d" % i, [128, 2, 6], F32) for i in range(NB)]
        self.mv_l = [sb("mv%d" % i, [128, 2], F32) for i in range(NB)]
        self.rstd_l = [sb("rstd%d" % i, [128, 1], F32) for i in range(NB)]
        self.nmr_l = [sb("nmr%d" % i, [128, 1], F32) for i in range(NB)]
        self.wctr = 0
        self.tctr = 0
        ident, identb = self.ident, self.identb
        P.op("pool", lambda g: g.memset(ident[:], 1.0), writes=["ident"])
        P.op("pool", lambda g: g.affine_select(out=ident[:], in_=ident[:], pattern=[[-1, 128]], compare_op=ALU.is_equal,
                                                fill=0.0, base=0, channel_multiplier=1), writes=["ident"])
        P.op("pool", lambda g: g.tensor_copy(out=identb[:], in_=ident[:]), reads=["ident"], writes=["identb"])

    def transpose_to(self, src, n, dst, col0, rsrc, rdst, banks, bnames):
        P, ident = self.P, self.ident
        for half in range(2):
            bank, bn = banks[half], bnames[half]
            for j in range(4):
                kc = half * 4 + j
                P.op("pe", lambda pe, kc=kc, j=j, bank=bank: pe.transpose(
                    out=bank[:, j * 128:j * 128 + n], in_=src[0:n, kc * 128:(kc + 1) * 128], identity=ident[0:n, 0:n]),
                    reads=[rsrc, "ident"], writes=[bn])
            P.op("act", lambda a, half=half, bank=bank: a.activation(
                out=dst[:, half * 4:half * 4 + 4, col0:col0 + n],
                in_=bank[:, :].rearrange("p (j c) -> p j c", j=4)[:, :, 0:n], func=AF.Copy),
                reads=[bn], writes=[rdst])

    def load_w_block(self, src):
        i = self.wctr % 2
        self.wctr += 1
        wbb = self.wbb
        for half in range(2):
            self.P.dma("pool", lambda g, half=half: g.dma_start(
                out=wbb[i][:, half * 4:half * 4 + 4, :], in_=src[:, half * 4:half * 4 + 4, :]), writes=["wbb%d" % i])
        return i

    def load_wout(self, w):
        wob = self.wob
        for half in range(2):
            for kh in range(2):
                self.P.dma("pool", lambda g, half=half, kh=kh: g.dma_start(
                    out=wob[:, kh * 4:kh * 4 + 4, half * 512:(half + 1) * 512],
                    in_=w.rearrange("(kc p) c -> p kc c", p=128)[:, kh * 4:kh * 4 + 4, half * 512:(half + 1) * 512]),
                    writes=["wob"])

    def load_ln(self, lng_row, lnb_row):
        gbt, bbt = self.gbt, self.bbt
        self.P.dma("sp", lambda s: s.dma_start(out=gbt[:], in_=lng_row.partition_broadcast(128)), writes=["gbt"])
        self.P.dma("sp", lambda s: s.dma_start(out=bbt[:], in_=lnb_row.partition_broadcast(128)), writes=["bbt"])

    def outproj_ln(self, n, lhs_fn, rlhs, xres_fn, out_fn, banks, bnames, defer=False):
        P = self.P
        i = self.tctr % self.NB
        self.tctr += 1
        wob, xt_ring, x1t = self.wob, self.xt_ring, self.x1t
        stats, mv, rstd, nmr, gbt, bbt = self.stats_l[i], self.mv_l[i], self.rstd_l[i], self.nmr_l[i], self.gbt, self.bbt
        rs_, rm_, rr_, rn_ = "stats%d" % i, "mv%d" % i, "rstd%d" % i, "nmr%d" % i
        for half in range(2):
            for fc in range(8):
                P.op("pe", lambda pe, half=half, fc=fc: pe.matmul(
                    out=banks[half][0:n, :], lhsT=lhs_fn(fc), rhs=wob[:, fc, half * 512:(half + 1) * 512],
                    start=(fc == 0), stop=(fc == 7)),
                    reads=[rlhs, "wob"], writes=[bnames[half]])
        if xres_fn is not None:
            xres_fn(i)
        for half in range(2):
            P.op("dve", lambda v, half=half: v.scalar_tensor_tensor(
                out=x1t[i][0:n, half * 512:(half + 1) * 512], in0=xt_ring[i][0:n, half * 512:(half + 1) * 512],
                scalar=ALPHA, in1=banks[half][0:n, :], op0=ALU.mult, op1=ALU.add),
                reads=["xt%d" % i, bnames[half]], writes=["x1t%d_h%d" % (i, half)], deps=self.P.res("x1t%d" % i).r + self.P.res("x1t%d" % i).w)
        for half in range(2):
            P.op("dve", lambda v, half=half: v.bn_stats(out=stats[0:n, half, :], in_=x1t[i][0:n, half * 512:(half + 1) * 512]),
                 reads=["x1t%d_h%d" % (i, half)], writes=[rs_ + "_%d" % half])
        P.op("dve", lambda v: v.bn_aggr(out=mv[0:n, :], in_=stats[0:n, :, :].rearrange("p a b -> p (a b)")),
             reads=[rs_ + "_0", rs_ + "_1"], writes=[rm_])
        P.op("act", lambda a: a.activation(out=rstd[0:n, :], in_=mv[0:n, 1:2], func=AF.Sqrt, bias=EPS, scale=1.0),
             reads=[rm_], writes=[rr_])
        P.op("dve", lambda v: v.reciprocal(out=rstd[0:n, :], in_=rstd[0:n, :]), reads=[rr_], writes=[rr_])
        P.op("dve", lambda v: v.scalar_tensor_tensor(out=nmr[0:n, :], in0=mv[0:n, 0:1], scalar=-1.0, in1=rstd[0:n, :],
                                                     op0=ALU.mult, op1=ALU.mult),
             reads=[rm_, rr_], writes=[rn_])
        P.op("act", lambda a: a.activation(out=x1t[i][0:n, :], in_=x1t[i][0:n, :], func=AF.Identity,
                                           bias=nmr[0:n, :], scale=rstd[0:n, :]),
             reads=[rn_, rr_, "x1t%d_h0" % i, "x1t%d_h1" % i, rs_ + "_0", rs_ + "_1"], writes=["x1t%d" % i, "x1t%d_h0" % i, "x1t%d_h1" % i])
        P.op("pool", lambda v: v.tensor_tensor(out=x1t[i][0:n, :], in0=x1t[i][0:n, :], in1=gbt[0:n, :], op=ALU.mult),
             reads=["gbt"], writes=["x1t%d" % i])
        P.op("pool", lambda v: v.tensor_tensor(out=x1t[i][0:n, :], in0=x1t[i][0:n, :], in1=bbt[0:n, :], op=ALU.add),
             reads=["bbt"], writes=["x1t%d" % i])
        if defer:
            return lambda: out_fn(i)
        out_fn(i)


def wblock_view(w, blk):
    return w[blk].rearrange("(kc p) c -> p kc c", p=128)


def _block_w(w):
    return np.ascontiguousarray(w.reshape(D, 4, 8, HD).transpose(2, 0, 1, 3).reshape(8, D, 4 * HD))


def build_A():
    nc = bass.Bass("TRN2", target_bir_lowering=False)

    def din(name, shape, dt=F32):
        return nc.dram_tensor(name, shape, dt, kind="ExternalInput").ap()

    def dout(name, shape, dt=F32):
        return nc.dram_tensor(name, shape, dt, kind="ExternalOutput").ap()

    xs = din("xs", [NSA, D])
    stc = din("stc", [NSA, 2, D])
    ptab = din("ptab", [NSA, NPG], I32)
    ck = din("ck", [NPHYS * NQ, CH])
    cv = din("cv", [NPHYS * NQ, CH])
    w_in0 = din("w_in0", [8, D, 512])
    wconv = din("wconv", [3, D])
    w_out0 = din("w_out0", [D, D])
    w1c = din("w1c", [D, 512])
    lng0 = din("lng0", [1, D])
    lnb0 = din("lnb0", [1, D])
    coss = din("coss", [1, 16])
    sins = din("sins", [1, 16])

    css = dout("css", [NSA, 2, D])
    xs1o = dout("xs1o", [NSA, D])
    ksm = dout("ksm", [NSA, HD])
    vsm = dout("vsm", [NSA, HD])
    gco = dout("gco", [NSA, HD])

    with contextlib.ExitStack() as st:
        def sb(name, shape, dt=F32):
            return st.enter_context(nc.sbuf_tensor(name, shape, dt))

        def ps(name, shape, dt=F32):
            return st.enter_context(nc.psum_tensor(name, shape, dt))

        P = Prog(nc)
        C = Common(nc, P, sb, ps)
        pb = [ps("pb%d" % i, [128, 512], F32) for i in range(8)]
        pbn = ["pb%d" % i for i in range(8)]

        xsT = sb("xsT", [128, 8, NSA], BF16)
        gsT = sb("gsT", [128, 8, NSA], BF16)
        st_sb = sb("st_sb", [NSA, 2, D], F32)
        wcb = sb("wcb", [NSA, 3, D], F32)
        gs = sb("gs", [NSA, D], F32)
        us = sb("us", [NSA, D], F32)
        pev = sb("pev", [NSA, 512], F32)
        szb = sb("szb", [NSA, 128], F32)
        cvb = sb("cvb", [NSA, 128], F32)
        tb = sb("tb", [NSA, 128], F32)
        cosS = sb("cosS", [NSA, 16], F32)
        sinS = sb("sinS", [NSA, 16], F32)
        qk = sb("qk", [NSA, 2, HD], F32)
        vn = sb("vn", [NSA, HD], F32)
        szs = sb("szs", [NSA, HD], F32)
        rts = sb("rts", [NSA, 4, 2, 16], F32)
        pti = sb("pti", [NSA, NPG], I32)
        ptf = sb("ptf", [NSA, NPG], F32)
        idxk = sb("idxk", [NSA, NPG, NQ], I32)
        idxkf = sb("idxkf", [NSA, NPG, NQ], F32)
        NKB = 3
        kch = [sb("kch%d" % i, [NSA, CH], F32) for i in range(NKB)]
        kcbig = sb("kcbig", [NSA, 2 * CH], BF16)
        kcb = [kcbig[:, 0:CH], kcbig[:, CH:2 * CH]]
        vch = kch
        prod = sb("prod", [NSA, 8 * HD], F32)
        prodf = sb("prodf", [NSA, CH], F32)
        prodb = kcb
        prodf2 = kcbig[:, :].bitcast(F32)
        prodv = kcb
        part = sb("part", [NSA, HD], F32)
        kmacc = sb("kmacc", [NSA, 8, HD], F32)
        gate = sb("gate", [NSA, 8], F32)
        g2 = sb("g2", [NSA, 8], F32)
        eqt = sb("eqt", [NSA, 8], F32)
        sel = sb("sel", [NSA, 8], F32)
        mx1 = sb("mx1", [NSA, 1], F32)
        cnt = sb("cnt", [NSA, 8], F32)
        oh = sb("oh", [NSA, 8], F32)
        ptmp = sb("ptmp", [NSA, 2, 8], F32)
        physf = sb("physf", [NSA, 3, 2], F32)
        idxsf = sb("idxsf", [NSA, 6, NQ], F32)
        idxs = sb("idxs", [NSA, 6, NQ], I32)
        sc = sb("sc", [NSA, 6 * 128 + 1], F32)
        pp = sb("pp", [NSA, 6 * 128 + 1], F32)
        smx = sb("smx", [NSA, 1], F32)
        lsum = sb("lsum", [NSA, 1], F32)
        oacc = sb("oacc", [NSA, HD], F32)

        P.dma("sp", lambda s: s.dma_start(out=C.xt_ring[0][:], in_=xs), writes=["xt0"])
        P.dma("sp", lambda s: s.dma_start(out=st_sb[:], in_=stc), writes=["st_sb"])
        for j in range(3):
            P.dma("sp", lambda s, j=j: s.dma_start(out=wcb[:, j, :], in_=wconv[j:j + 1, :].partition_broadcast(NSA)), writes=["wcb"])
        P.dma("sp", lambda s: s.dma_start(out=cosS[:], in_=coss.partition_broadcast(NSA)), writes=["cosS"])
        P.dma("sp", lambda s: s.dma_start(out=sinS[:], in_=sins.partition_broadcast(NSA)), writes=["sinS"])
        P.dma("sp", lambda s: s.dma_start(out=pti[:], in_=ptab), writes=["pti"])
        P.dma("sp", lambda s: s.dma_start(out=css[:, 0, :], in_=st_sb[:, 1, :]), reads=["st_sb"])

        P.op("dve", lambda v: v.tensor_copy(out=ptf[:], in_=pti[:]), reads=["pti"], writes=["ptf"])
        for tq in range(NQ):
            P.op("dve", lambda v, tq=tq: v.tensor_scalar(out=idxkf[:, :, tq], in0=ptf[:], scalar1=float(NQ), scalar2=float(tq),
                                                         op0=ALU.mult, op1=ALU.add), reads=["ptf"], writes=["idxkf"])
        P.op("dve", lambda v: v.tensor_copy(out=idxk[:], in_=idxkf[:]), reads=["idxkf"], writes=["idxk"])

        def kstream():
            n_ch = 0
            for pg in range(NPG):
                n = pg // 2
                for tq in range(NQ):
                    b = n_ch % NKB
                    n_ch += 1
                    P.dma("pool", lambda g, b=b, pg=pg, tq=tq: g.indirect_dma_start(
                        out=kch[b][:], out_offset=None, in_=ck,
                        in_offset=bass.IndirectOffsetOnAxis(ap=idxk[:, pg, tq:tq + 1], axis=0)),
                        reads=["idxk"], writes=["kch%d" % b])
                    first = (pg % 2 == 0 and tq == 0)
                    dst = kmacc[:, n, :] if first else part[:]
                    P.op("dve", lambda v, b=b, dst=dst: v.tensor_reduce(
                        out=dst, in_=kch[b][:, :].rearrange("p (t d) -> p d t", d=HD), axis=AX.X, op=ALU.add),
                        reads=["kch%d" % b], writes=["kmacc" if first else "part"])
                    if not first:
                        P.op("dve", lambda v, n=n: v.tensor_tensor(out=kmacc[:, n, :], in0=kmacc[:, n, :], in1=part[:], op=ALU.add),
                             reads=["part"], writes=["kmacc"])
                    yield

        ks = kstream()

        def pump(k):
            for _ in range(k):
                try:
                    next(ks)
                except StopIteration:
                    return

        import os
        STOP = int(os.environ.get('K_STOP', '99'))
        pump(5)
        if STOP <= 0:
            pump(1000); P.emit(); return nc
        C.transpose_to(C.xt_ring[0], NSA, xsT, 0, "xt0", "xsT", [pb[0], pb[1]], pbn[0:2])
        for fc in range(8):
            wi = C.load_w_block(wblock_view(w_in0, fc))
            bank, bn = pb[2 + fc % 2], pbn[2 + fc % 2]
            for kc in range(8):
                P.op("pe", lambda pe, kc=kc, wi=wi, bank=bank: pe.matmul(out=bank[:, :], lhsT=xsT[:, kc, :], rhs=C.wbb[wi][:, kc, :],
                                                                         start=(kc == 0), stop=(kc == 7)),
                     reads=["xsT", "wbb%d" % wi], writes=[bn])
            cols = slice(fc * 128, (fc + 1) * 128)
            P.op("act", lambda a, bank=bank: a.activation(out=pev[:], in_=bank[:, :], func=AF.Copy), reads=[bn], writes=["pev"])
            pB, pC, pH, pZ = pev[:, 0:128], pev[:, 128:256], pev[:, 256:384], pev[:, 384:512]
            P.op("act", lambda a, pZ=pZ: a.activation(out=szb[:], in_=pZ, func=AF.Silu), reads=["pev"], writes=["szb"])
            P.op("dve", lambda v, pC=pC, pH=pH, cols=cols: v.tensor_tensor(out=us[:, cols], in0=pC, in1=pH, op=ALU.mult),
                 reads=["pev"], writes=["us"])
            P.op("dve", lambda v, cols=cols: v.tensor_tensor(out=cvb[:], in0=us[:, cols], in1=wcb[:, 2, cols], op=ALU.mult),
                 reads=["us", "wcb"], writes=["cvb"])
            P.op("dve", lambda v, cols=cols: v.tensor_tensor(out=tb[:], in0=st_sb[:, 1, cols], in1=wcb[:, 1, cols], op=ALU.mult),
                 reads=["st_sb", "wcb"], writes=["tb"])
            P.op("dve", lambda v: v.tensor_tensor(out=cvb[:], in0=cvb[:], in1=tb[:], op=ALU.add), reads=["tb"], writes=["cvb"])
            P.op("dve", lambda v, cols=cols: v.tensor_tensor(out=tb[:], in0=st_sb[:, 0, cols], in1=wcb[:, 0, cols], op=ALU.mult),
                 reads=["st_sb", "wcb"], writes=["tb"])
            P.op("dve", lambda v: v.tensor_tensor(out=cvb[:], in0=cvb[:], in1=tb[:], op=ALU.add), reads=["tb"], writes=["cvb"])
            P.op("dve", lambda v: v.tensor_tensor(out=cvb[:], in0=cvb[:], in1=szb[:], op=ALU.mult), reads=["szb"], writes=["cvb"])
            P.op("dve", lambda v, pB=pB, cols=cols: v.tensor_tensor(out=gs[:, cols], in0=pB, in1=cvb[:], op=ALU.mult),
                 reads=["pev", "cvb"], writes=["gs"])
            pump(3)
        P.dma("sp", lambda s: s.dma_start(out=css[:, 1, :], in_=us[:]), reads=["us"])
        C.transpose_to(gs, NSA, gsT, 0, "gs", "gsT", [pb[0], pb[1]], pbn[0:2])
        C.load_wout(w_out0)
        C.load_ln(lng0, lnb0)

        def xres(i):
            P.dma("sp", lambda s: s.dma_start(out=C.xt_ring[i][:], in_=xs), writes=["xt%d" % i])

        xi = [0]

        def outf(i):
            xi[0] = i
            P.dma("sp", lambda s: s.dma_start(out=xs1o, in_=C.x1t[i][:]), reads=["x1t%d" % i])
            C.transpose_to(C.x1t[i], NSA, xsT, 0, "x1t%d" % i, "xsT", [pb[0], pb[1]], pbn[0:2])
        C.outproj_ln(NSA, lambda fc: gsT[:, fc, :], "gsT", xres, outf, [pb[4], pb[5]], pbn[4:6])
        pump(6)
        if STOP <= 1:
            pump(1000); P.emit(); return nc

        wi = C.wctr % 2
        C.wctr += 1
        for half in range(2):
            P.dma("pool", lambda g, half=half: g.dma_start(
                out=C.wbb[wi][:, half * 4:half * 4 + 4, :],
                in_=w1c.rearrange("(kc p) c -> p kc c", p=128)[:, half * 4:half * 4 + 4, :]), writes=["wbb%d" % wi])
        SUB = int(os.environ.get("K_SUB", "99"))
        if SUB <= 1:
            pump(1000); P.emit(); return nc
        bank, bn = pb[2], pbn[2]
        for kc in range(8):
            P.op("pe", lambda pe, kc=kc: pe.matmul(out=bank[:, :], lhsT=xsT[:, kc, :], rhs=C.wbb[wi][:, kc, :],
                                                   start=(kc == 0), stop=(kc == 7)),
                 reads=["xsT", "wbb%d" % wi], writes=[bn])
        if SUB <= 2:
            pump(1000); P.emit(); return nc
        qk_ps = bank[:, 0:256].rearrange("p (a c) -> p a c", a=2)
        qraw = sb("qraw", [NSA, 2, HD], F32)
        P.op("act", lambda a: a.activation(out=qraw[:], in_=qk_ps, func=AF.Copy), reads=[bn], writes=["qraw"])
        P.op("act", lambda a: a.activation(out=qk[:, :, 32:128], in_=qk_ps[:, :, 32:128], func=AF.Copy), reads=[bn], writes=["qk_hi"])
        P.op("act", lambda a: a.activation(out=vn[:], in_=bank[:, 256:384], func=AF.Copy), reads=[bn], writes=["vn"])
        P.op("act", lambda a: a.activation(out=szs[:], in_=bank[:, 384:512], func=AF.Silu), reads=[bn], writes=["szs"])
        if SUB <= 3:
            pump(1000); P.emit(); return nc
        cs = cosS[:, :].unsqueeze(1).to_broadcast([NSA, 2, 16])
        sn = sinS[:, :].unsqueeze(1).to_broadcast([NSA, 2, 16])
        x1v, x2v = qraw[:, :, 0:16], qraw[:, :, 16:32]
        P.op("dve", lambda v: v.tensor_tensor(out=rts[:, 0], in0=x1v, in1=cs, op=ALU.mult), reads=["qraw", "cosS"], writes=["rts"])
        P.op("dve", lambda v: v.tensor_tensor(out=rts[:, 1], in0=x2v, in1=sn, op=ALU.mult), reads=["qraw", "sinS"], writes=["rts"])
        P.op("dve", lambda v: v.tensor_tensor(out=rts[:, 2], in0=x2v, in1=cs, op=ALU.mult), reads=["qraw"], writes=["rts"])
        P.op("dve", lambda v: v.tensor_tensor(out=rts[:, 3], in0=x1v, in1=sn, op=ALU.mult), reads=["qraw"], writes=["rts"])
        P.op("dve", lambda v: v.tensor_tensor(out=qk[:, :, 0:16], in0=rts[:, 0], in1=rts[:, 1], op=ALU.subtract), reads=["rts"], writes=["qk_lo"])
        P.op("dve", lambda v: v.tensor_tensor(out=qk[:, :, 16:32], in0=rts[:, 2], in1=rts[:, 3], op=ALU.add), reads=["rts"], writes=["qk_lo"])
        if not os.environ.get("SKIP_OUT"):
            P.dma("sp", lambda s: s.dma_start(out=ksm, in_=qk[:, 1, :]), reads=["qk_lo", "qk_hi"])
            P.dma("sp", lambda s: s.dma_start(out=vsm, in_=vn[:]), reads=["vn"])
        pump(1000)
        if STOP <= 2:
            P.emit(); return nc

        qb8 = qk[:, 0, :].unsqueeze(1).to_broadcast([NSA, 8, HD])
        P.op("dve", lambda v: v.tensor_tensor(out=prod[:, 0:8 * HD].rearrange("p (n d) -> p n d", n=8), in0=kmacc[:], in1=qb8, op=ALU.mult),
             reads=["kmacc", "qk_lo", "qk_hi"], writes=["prod"])
        P.op("dve", lambda v: v.tensor_reduce(out=gate[:], in_=prod[:, 0:8 * HD].rearrange("p (n d) -> p n d", n=8), axis=AX.X, op=ALU.add),
             reads=["prod"], writes=["gate"])

        def bc(t):
            return t[:, 0:1].to_broadcast([NSA, 8])
        P.op("dve", lambda v: v.tensor_reduce(out=mx1[:], in_=gate[:], axis=AX.X, op=ALU.max), reads=["gate"], writes=["mx1"])
        P.op("dve", lambda v: v.tensor_tensor(out=eqt[:], in0=gate[:], in1=bc(mx1), op=ALU.is_ge), reads=["gate", "mx1"], writes=["eqt"])
        P.op("dve", lambda v: v.scalar_tensor_tensor(out=g2[:], in0=eqt[:], scalar=-BIG, in1=gate[:], op0=ALU.mult, op1=ALU.add),
             reads=["eqt", "gate"], writes=["g2"])
        P.op("dve", lambda v: v.tensor_reduce(out=mx1[:], in_=g2[:], axis=AX.X, op=ALU.max), reads=["g2"], writes=["mx1"])
        P.op("dve", lambda v: v.tensor_tensor(out=eqt[:], in0=g2[:], in1=bc(mx1), op=ALU.is_ge), reads=["g2", "mx1"], writes=["eqt"])
        P.op("dve", lambda v: v.scalar_tensor_tensor(out=g2[:], in0=eqt[:], scalar=-BIG, in1=g2[:], op0=ALU.mult, op1=ALU.add),
             reads=["eqt"], writes=["g2"])
        P.op("dve", lambda v: v.tensor_reduce(out=mx1[:], in_=g2[:], axis=AX.X, op=ALU.max), reads=["g2"], writes=["mx1"])
        P.op("dve", lambda v: v.tensor_tensor(out=sel[:], in0=gate[:], in1=bc(mx1), op=ALU.is_ge), reads=["gate", "mx1"], writes=["sel"])
        P.op("dve", lambda v: v.memset(cnt[:, 0:1], 0.0), writes=["cnt"])
        for n in range(1, 8):
            P.op("dve", lambda v, n=n: v.tensor_tensor(out=cnt[:, n:n + 1], in0=cnt[:, n - 1:n], in1=sel[:, n - 1:n], op=ALU.add),
                 reads=["sel"], writes=["cnt"])
        ptv = ptf[:, :].rearrange("p (n a) -> p a n", a=2)
        for j in range(3):
            P.op("dve", lambda v, j=j: v.tensor_scalar(out=oh[:], in0=cnt[:], scalar1=float(j), scalar2=None, op0=ALU.is_equal),
                 reads=["cnt"], writes=["oh"])
            P.op("dve", lambda v: v.tensor_tensor(out=oh[:], in0=oh[:], in1=sel[:], op=ALU.mult), reads=["sel"], writes=["oh"])
            P.op("dve", lambda v: v.tensor_tensor(out=ptmp[:], in0=ptv, in1=oh[:, :].unsqueeze(1).to_broadcast([NSA, 2, 8]), op=ALU.mult),
                 reads=["oh", "ptf"], writes=["ptmp"])
            P.op("dve", lambda v, j=j: v.tensor_reduce(out=physf[:, j, :], in_=ptmp[:], axis=AX.X, op=ALU.add),
                 reads=["ptmp"], writes=["physf"])
        for tq in range(NQ):
            P.op("dve", lambda v, tq=tq: v.tensor_scalar(out=idxsf[:, :, tq], in0=physf[:, :, :].rearrange("p j a -> p (j a)"),
                                                         scalar1=float(NQ), scalar2=float(tq), op0=ALU.mult, op1=ALU.add),
                 reads=["physf"], writes=["idxsf"])
        P.op("dve", lambda v: v.tensor_copy(out=idxs[:], in_=idxsf[:]), reads=["idxsf"], writes=["idxs"])

        if STOP <= 3:
            P.emit(); return nc
        qbt = qk[:, 0, :].unsqueeze(1).to_broadcast([NSA, TPC, HD])
        NG = CH // 512
        n_ch = 0
        for ja in range(6):
            for tq in range(NQ):
                b = n_ch % NKB
                pbi = n_ch % 2
                n_ch += 1
                P.dma("pool", lambda g, b=b, ja=ja, tq=tq: g.indirect_dma_start(
                    out=kch[b][:], out_offset=None, in_=ck,
                    in_offset=bass.IndirectOffsetOnAxis(ap=idxs[:, ja, tq:tq + 1], axis=0)),
                    reads=["idxs"], writes=["kch%d" % b])
                pf, pfn = (prodf, "prodf") if n_ch % 2 == 0 else (prodf2, "prodf2")
                meng = "pool" if n_ch % 3 == 0 else "dve"
                P.op(meng, lambda v, b=b, pf=pf: v.tensor_tensor(out=pf[:, :].rearrange("p (t d) -> p t d", d=HD),
                                                                in0=kch[b][:, :].rearrange("p (t d) -> p t d", d=HD), in1=qbt, op=ALU.mult),
                     reads=["kch%d" % b, "qk_lo", "qk_hi"], writes=[pfn])
                c0 = ja * 128 + tq * TPC
                P.op("dve", lambda v, c0=c0, pf=pf: v.tensor_reduce(out=sc[:, c0:c0 + TPC], in_=pf[:, :].rearrange("p (t d) -> p t d", d=HD),
                                                                    axis=AX.X, op=ALU.add), reads=[pfn], writes=["sc_%d" % (n_ch % 2)])
        P.op("dve", lambda v: v.tensor_tensor(out=part[:], in0=qk[:, 0, :], in1=qk[:, 1, :], op=ALU.mult),
             reads=["qk_lo", "qk_hi"], writes=["part"])
        P.op("dve", lambda v: v.tensor_reduce(out=sc[:, 768:769], in_=part[:], axis=AX.X, op=ALU.add), reads=["part"], writes=["sc"])
        P.op("dve", lambda v: v.tensor_reduce(out=smx[:], in_=sc[:], axis=AX.X, op=ALU.max), reads=["sc", "sc_0", "sc_1"], writes=["smx"])
        P.op("dve", lambda v: v.tensor_scalar(out=smx[:], in0=smx[:], scalar1=-SCALE, scalar2=None, op0=ALU.mult), writes=["smx"])
        P.op("act", lambda a: a.activation(out=pp[:], in_=sc[:], func=AF.Exp, bias=smx[:], scale=SCALE, accum_out=lsum[:]),
             reads=["sc", "sc_0", "sc_1", "smx"], writes=["pp", "lsum"])

        if STOP <= 4:
            P.emit(); return nc
        n_ch = 0
        tot = 6 * NQ
        for ja in range(6):
            for tq in range(NQ):
                b = n_ch % NKB
                pvi = n_ch % 2
                n_ch += 1
                P.dma("pool", lambda g, b=b, ja=ja, tq=tq: g.indirect_dma_start(
                    out=vch[b][:], out_offset=None, in_=cv,
                    in_offset=bass.IndirectOffsetOnAxis(ap=idxs[:, ja, tq:tq + 1], axis=0)),
                    reads=["idxs"], writes=["kch%d" % b])
                c0 = ja * 128 + tq * TPC
                meng = "dve" if (n_ch % 5) < 3 else "pool"
                P.op(meng, lambda g, b=b, c0=c0, pvi=pvi: g.tensor_tensor(
                    out=prodv[pvi][:, :].rearrange("p (t d) -> p t d", d=HD), in0=vch[b][:, :].rearrange("p (t d) -> p t d", d=HD),
                    in1=pp[:, c0:c0 + TPC].unsqueeze(2).to_broadcast([NSA, TPC, HD]), op=ALU.mult),
                    reads=["kch%d" % b, "pp"], writes=["kcb%d" % pvi])
                for gi in range(NG):
                    first = (n_ch == 1 and gi == 0)
                    last = (n_ch == tot and gi == NG - 1)
                    P.op("pe", lambda pe, pvi=pvi, gi=gi, first=first, last=last: pe.matmul(
                        out=pb[4][:, :], lhsT=C.identb[:], rhs=prodv[pvi][:, gi * 512:(gi + 1) * 512], start=first, stop=last),
                        reads=["kcb%d" % pvi, "identb"], writes=[pbn[4]])
        P.op("dve", lambda v: v.tensor_reduce(out=oacc[:], in_=pb[4][:, :].rearrange("p (g d) -> p d g", d=HD), axis=AX.X, op=ALU.add),
             reads=[pbn[4]], writes=["oacc"])
        P.op("dve", lambda v: v.scalar_tensor_tensor(out=oacc[:], in0=vn[:], scalar=pp[:, 768:769], in1=oacc[:], op0=ALU.mult, op1=ALU.add),
             reads=["vn", "pp"], writes=["oacc"])
        P.op("dve", lambda v: v.reciprocal(out=lsum[:], in_=lsum[:]), reads=["lsum"], writes=["lsum"])
        P.op("dve", lambda v: v.scalar_tensor_tensor(out=oacc[:], in0=oacc[:], scalar=lsum[:, 0:1], in1=szs[:], op0=ALU.mult, op1=ALU.mult),
             reads=["lsum", "szs"], writes=["oacc"])
        P.dma("sp", lambda s: s.dma_start(out=gco, in_=oacc[:]), reads=["oacc"])
        P.emit()
    return nc


def build_B():
    nc = bass.Bass("TRN2", target_bir_lowering=False)

    def din(name, shape, dt=F32):
        return nc.dram_tensor(name, shape, dt, kind="ExternalInput").ap()

    def dout(name, shape, dt=F32):
        return nc.dram_tensor(name, shape, dt, kind="ExternalOutput").ap()

    xp = din("xp", [T, D])
    gsm = din("gsm", [NS, D])
    xs1m = din("xs1m", [NS, D])
    w_in0 = din("w_in0", [8, D, 512])
    wconv = din("wconv", [3, D])
    w_out0 = din("w_out0", [D, D])
    w_in1 = din("w_in1", [8, D, 512])
    w_out1 = din("w_out1", [D, D])
    lng = din("lng", [2, D])
    lnb = din("lnb", [2, D])
    cosp = din("cosp", [T, 16])
    sinp = din("sinp", [T, 16])

    yp = dout("yp", [T, D])
    ys = dout("ys", [NS, D])
    csp = dout("csp", [2, D])
    kp = dout("kp", [T, D])
    vp = dout("vp", [T, D])
    x1d = nc.dram_tensor("x1d", [T, D], F32, kind="Internal").ap()

    with contextlib.ExitStack() as st:
        def sb(name, shape, dt=F32):
            return st.enter_context(nc.sbuf_tensor(name, shape, dt))

        def ps(name, shape, dt=F32):
            return st.enter_context(nc.psum_tensor(name, shape, dt))

        P = Prog(nc)
        C = Common(nc, P, sb, ps, nb=4)
        pb = [ps("pb%d" % i, [128, 512], F32) for i in range(7)]
        pbn = ["pb%d" % i for i in range(7)]
        pbt = ps("pbt", [128, 1024], BF16)

        cst = C.x1t[3]
        onesb = sb("onesb", [128, 128], BF16)
        trib = sb("trib", [128, 128], BF16)
        z8 = sb("z8", [8, 8, 128], BF16)
        xT = sb("xT", [128, 8, T], BF16)
        gT = sb("gT", [128, 8, T], BF16)
        wcT = sb("wcT", [128, 8, 3], F32)
        scrA = sb("scrA", [128, T + 2], F32)
        scrB = sb("scrB", [128, 2048], F32)
        uT = scrA
        szT = scrA
        hsb, szb, gbb, cvb = (scrB[:, 0:512], scrB[:, 512:1024], scrB[:, 1024:1536], scrB[:, 1536:2048])
        qT = scrB[:, 0:1024].bitcast(BF16)
        kT = scrB[:, 1024:2048].bitcast(BF16)
        vb = sb("vb", [128, 16, 128], BF16)
        cosP = sb("cosP", [128, 16, 16], F32)
        sinP = sb("sinP", [128, 16, 16], F32)
        qkf = [sb("qkf%d" % i, [128, 2, 128], F32) for i in range(4)]
        qkb = [sb("qkb%d" % i, [128, 2, 128], BF16) for i in range(4)]
        vf = [sb("vf%d" % i, [128, 128], F32) for i in range(4)]
        rt = [sb("rt%d" % i, [128, 4, 2, 16], F32) for i in range(4)]
        kmT = sb("kmT", [128, 8], F32)
        kmTb = sb("kmTb", [128, 8], BF16)
        candb = sb("candb", [128, 8, 8], F32)
        gm = sb("gm", [128, 8, 8], F32)
        g2 = sb("g2", [128, 8, 8], F32)
        eqt = sb("eqt", [128, 8, 8], F32)
        mx1 = sb("mx1", [128, 8], F32)
        negb = sb("negb", [128, 16, 8], BF16)
        negbT = sb("negbT", [8, T], BF16)
        pT = [sb("pT%d" % i, [128, 2, 256], BF16) for i in range(3)]
        rl = sb("rl", [128, 256], F32)
        otmp = sb("otmp", [128, 256], F32)
        gsT = sb("gsT", [128, 8, NS], BF16)

        P.op("pool", lambda g: g.memset(onesb[:], 1.0), writes=["onesb"])
        P.op("pool", lambda g: g.memset(cst[:, 0:128], 0.0), writes=["x1t3"])
        P.op("pool", lambda g: g.affine_select(out=cst[:, 0:128], in_=cst[:, 0:128], pattern=[[1, 128]], compare_op=ALU.is_ge,
                                                fill=NEG, base=0, channel_multiplier=-1), writes=["x1t3"])
        P.op("pool", lambda g: g.tensor_copy(out=trib[:], in_=cst[:, 0:128]), reads=["x1t3"], writes=["trib"])
        z8v = cst[0:8, :].rearrange("p (n c) -> p n c", n=8)
        P.op("pool", lambda g: g.memset(cst[0:8, :], 1.0), writes=["x1t3"])
        P.op("pool", lambda g: g.affine_select(out=z8v, in_=z8v, pattern=[[1, 8], [0, 128]], compare_op=ALU.is_equal,
                                                fill=0.0, base=0, channel_multiplier=-1), writes=["x1t3"])
        P.op("pool", lambda g: g.tensor_copy(out=z8[:], in_=z8v), reads=["x1t3"], writes=["z8"])
        P.op("pool", lambda g: g.memset(candb[:], 0.0), writes=["candb"])
        for j in range(8):
            qb = 4 + j // 2
            P.op("pool", lambda g, j=j, qb=qb: g.affine_select(out=candb[:, j, :], in_=candb[:, j, :], pattern=[[-1, 8]],
                                                               compare_op=ALU.is_ge, fill=-BIG, base=qb - 1, channel_multiplier=0),
                 writes=["candb"])
        nbs = cst[:, 0:64].rearrange("p (t n) -> p t n", t=8)
        P.op("pool", lambda g: g.memset(cst[:, 0:64], 0.0), reads=["x1t3"], writes=["x1t3"])
        for j in range(8):
            qb = j // 2
            P.op("pool", lambda g, j=j, qb=qb: g.affine_select(out=nbs[:, j, :], in_=nbs[:, j, :], pattern=[[-1, 8]],
                                                               compare_op=ALU.is_ge, fill=NEG, base=qb - 1, channel_multiplier=0),
                 writes=["x1t3"])
        P.op("pool", lambda g: g.tensor_copy(out=negb[:, 0:8, :], in_=nbs), reads=["x1t3"], writes=["negb_s"])
        P.op("pool", lambda g: g.memset(uT[:, 0:2], 0.0), writes=["scrA"])

        for j in range(3):
            P.dma("sp", lambda s, j=j: s.dma_start(out=wcT[:, :, j], in_=wconv[j, :].rearrange("(fc p) -> p fc", p=128),
                                                   allow_slow_non_contiguous=True), writes=["wcT"])
        P.dma("sp", lambda s: s.dma_start(out=cosP[:], in_=cosp.rearrange("(t p) i -> p t i", p=128)), writes=["cosP"])
        P.dma("sp", lambda s: s.dma_start(out=sinP[:], in_=sinp.rearrange("(t p) i -> p t i", p=128)), writes=["sinP"])

        def a1_tiles():
            for tt in range(16):
                i = tt % 3
                P.dma("sp", lambda s, tt=tt, i=i: s.dma_start(out=C.xt_ring[i][:], in_=xp[tt * 128:(tt + 1) * 128, :]), writes=["xt%d" % i])
                C.transpose_to(C.xt_ring[i], 128, xT, tt * 128, "xt%d" % i, "xT_%d" % tt, [pb[4], pb[5]], pbn[4:6])
                yield

        a1 = a1_tiles()
        a1_done = [0]

        def a1_pump(upto):
            while a1_done[0] < min(upto, 16):
                next(a1)
                a1_done[0] += 1

        a1_pump(4)
        for fc in range(8):
            wi = C.load_w_block(wblock_view(w_in0, fc))
            for tt in range(4):
                a1_pump(4 * tt + 8)
                cols = slice(tt * 512, (tt + 1) * 512)
                for g, bi in ((2, 2), (3, 3), (1, 1), (0, 0)):
                    for kc in range(8):
                        P.op("pe", lambda pe, g=g, bi=bi, kc=kc, wi=wi, cols=cols: pe.matmul(
                            out=pb[bi][:, :], lhsT=C.wbb[wi][:, kc, g * 128:(g + 1) * 128], rhs=xT[:, kc, cols],
                            start=(kc == 0), stop=(kc == 7)),
                            reads=["wbb%d" % wi] + ["xT_%d" % t2 for t2 in range(tt * 4, tt * 4 + 4)], writes=[pbn[bi]])
                P.op("act", lambda a: a.activation(out=hsb, in_=pb[2][:, :], func=AF.Copy), reads=[pbn[2]], writes=["hsb"])
                P.op("act", lambda a: a.activation(out=szb, in_=pb[3][:, :], func=AF.Silu), reads=[pbn[3]], writes=["szb"])
                uo = 2 + tt * 512
                P.op("dve", lambda v, uo=uo: v.tensor_tensor(out=uT[:, uo:uo + 512], in0=pb[1][:, :], in1=hsb, op=ALU.mult),
                     reads=[pbn[1], "hsb"], writes=["scrA"])
                P.op("dve", lambda v, uo=uo, fc=fc: v.tensor_scalar(out=cvb, in0=uT[:, uo:uo + 512], scalar1=wcT[:, fc, 2:3], scalar2=None,
                                                                     op0=ALU.mult), reads=["scrA", "wcT"], writes=["cvb"])
                P.op("dve", lambda v, uo=uo, fc=fc: v.scalar_tensor_tensor(out=cvb, in0=uT[:, uo - 1:uo + 511], scalar=wcT[:, fc, 1:2], in1=cvb,
                                                                            op0=ALU.mult, op1=ALU.add), reads=["scrA"], writes=["cvb"])
                P.op("dve", lambda v, uo=uo, fc=fc: v.scalar_tensor_tensor(out=cvb, in0=uT[:, uo - 2:uo + 510], scalar=wcT[:, fc, 0:1], in1=cvb,
                                                                            op0=ALU.mult, op1=ALU.add), reads=["scrA"], writes=["cvb"])
                P.op("dve", lambda v: v.tensor_tensor(out=gbb, in0=pb[0][:, :], in1=szb, op=ALU.mult), reads=[pbn[0], "szb"], writes=["gbb"])
                P.op("dve", lambda v, fc=fc, cols=cols: v.tensor_tensor(out=gT[:, fc, cols], in0=gbb, in1=cvb, op=ALU.mult),
                     reads=["gbb", "cvb"], writes=["gT"])
            P.dma("sp", lambda s, fc=fc: s.dma_start(out=csp.rearrange("j (f p) -> p f j", p=128)[:, fc, :], in_=uT[:, T:T + 2],
                                                      allow_slow_non_contiguous=True), reads=["scrA"])

        C.load_wout(w_out0)
        C.load_ln(lng[0:1, :], lnb[0:1, :])
        def xres0(tt):
            i = (base0 + tt) % C.NB
            P.dma("sp", lambda s: s.dma_start(out=C.xt_ring[i][:], in_=xp[tt * 128:(tt + 1) * 128, :]), writes=["xt%d" % i])

        base0 = C.tctr
        xres0(0)
        xres0(1)
        pendq = []
        for tt in range(16):
            def outf(i, tt=tt):
                P.dma("sp", lambda s: s.dma_start(out=x1d[tt * 128:(tt + 1) * 128, :], in_=C.x1t[i][:]),
                      reads=["x1t%d" % i], writes=["x1d_%d" % tt])
                C.transpose_to(C.x1t[i], 128, xT, tt * 128, "x1t%d" % i, "xT_%d" % tt, [pb[4], pb[5]], pbn[4:6])
            bsel = (tt % 2) * 2
            nxt = C.outproj_ln(128, lambda fc, tt=tt: gT[:, fc, tt * 128:(tt + 1) * 128], "gT", None, outf,
                               [pb[bsel], pb[bsel + 1]], pbn[bsel:bsel + 2], defer=True)
            if tt + 2 < 16:
                xres0(tt + 2)
            pendq.append(nxt)
            if len(pendq) > 3:
                pendq.pop(0)()
        for f_ in pendq:
            f_()

        pS = [pb[0], pb[1]]
        pSn = pbn[0:2]
        pO, pL, pOn, pLn = pb[2], pb[3], pbn[2], pbn[3]
        pQKV, pQn = [pb[4], pb[5]], pbn[4:6]
        pM, pMn = pb[6], pbn[6]
        szT2 = sb("szT2", [128, T], F32)
        qk2 = sb("qk2", [128, 2048], F32)
        vb2 = sb("vb2", [128, 16, 128], BF16)
        negb2 = sb("negb2", [128, 16, 8], BF16)
        negbT2 = sb("negbT2", [8, T], BF16)
        P.op("pool", lambda g: g.tensor_copy(out=negb2[:, 0:8, :], in_=nbs), reads=["x1t3"], writes=["negb_s1"])
        szTs = [szT, szT2]
        szTn = ["scrA", "szT2"]
        qTs = [qT, qk2[:, 0:1024].bitcast(BF16)]
        kTs = [kT, qk2[:, 1024:2048].bitcast(BF16)]
        vbs = [vb, vb2]
        negbs = [negb, negb2]
        negbTs = [negbT, negbT2]

        def proj_steps(h):
            par = h % 2
            szT_, qT_, kT_, vb_, negb_, negbT_ = szTs[par], qTs[par], kTs[par], vbs[par], negbs[par], negbTs[par]
            rq, rk, rv, rz, rnb, rnbT = "qT%d" % par, "kT%d" % par, "vb%d" % par, szTn[par], "negb_d%d" % par, "negbT%d" % par
            wi = C.load_w_block(wblock_view(w_in1, h))
            for tt in range(4):
                bank, bn = pQKV[tt % 2], pQn[tt % 2]
                cols = slice(tt * 512, (tt + 1) * 512)
                for kc in range(8):
                    P.op("pe", lambda pe, kc=kc, bank=bank, cols=cols, wi=wi: pe.matmul(
                        out=bank[:, :], lhsT=C.wbb[wi][:, kc, 384:512], rhs=xT[:, kc, cols], start=(kc == 0), stop=(kc == 7)),
                        reads=["wbb%d" % wi] + ["xT_%d" % t2 for t2 in range(tt * 4, tt * 4 + 4)], writes=[bn])
                P.op("act", lambda a, bank=bank, cols=cols: a.activation(out=szT_[:, cols], in_=bank[:, :], func=AF.Silu),
                     reads=[bn], writes=[rz])
                yield

            def emit_tr(tt):
                j = tt % 4
                t4 = tt % 4
                for a in range(2):
                    P.op("pe", lambda pe, a=a, j=j, t4=t4: pe.transpose(
                        out=pbt[:, a * 512 + t4 * 128:a * 512 + (t4 + 1) * 128], in_=qkb[j][:, a, :], identity=C.identb[:]),
                        reads=["qkb%d" % j, "identb"], writes=["pbt"])
                if t4 == 3:
                    c0 = (tt - 3) * 128
                    P.op("act", lambda a, c0=c0: a.activation(out=qT_[:, c0:c0 + 512], in_=pbt[:, 0:512], func=AF.Copy),
                         reads=["pbt"], writes=[rq])
                    P.op("act", lambda a, c0=c0: a.activation(out=kT_[:, c0:c0 + 512], in_=pbt[:, 512:1024], func=AF.Copy),
                         reads=["pbt"], writes=[rk])

            for tt in range(16):
                bank, bn = pQKV[tt % 2], pQn[tt % 2]
                j = tt % 4
                for kc in range(8):
                    P.op("pe", lambda pe, kc=kc, bank=bank, tt=tt, wi=wi: pe.matmul(
                        out=bank[:, 0:384], lhsT=xT[:, kc, tt * 128:(tt + 1) * 128], rhs=C.wbb[wi][:, kc, 0:384],
                        start=(kc == 0), stop=(kc == 7)), reads=["wbb%d" % wi, "xT_%d" % tt], writes=[bn])
                qk_ps = bank[:, 0:256].rearrange("p (a c) -> p a c", a=2)
                P.op("act", lambda a, qk_ps=qk_ps, j=j: a.activation(out=qkf[j][:], in_=qk_ps, func=AF.Copy),
                     reads=[bn], writes=["qkf%d_hi" % j, "qkf%d_lo" % j, "qkf%d_lo2" % j])
                P.op("act", lambda a, bank=bank, j=j: a.activation(out=vf[j][:], in_=bank[:, 256:384], func=AF.Copy),
                     reads=[bn], writes=["vf%d" % j])
                P.op("pool", lambda g, j=j, tt=tt: g.tensor_copy(out=vb_[:, tt, :], in_=vf[j][:]),
                     reads=["vf%d" % j], writes=[rv])
                cs = cosP[:, tt, :].unsqueeze(1).to_broadcast([128, 2, 16])
                sn = sinP[:, tt, :].unsqueeze(1).to_broadcast([128, 2, 16])
                x1v, x2v = qkf[j][:, :, 0:16], qkf[j][:, :, 16:32]
                rtr = "rt%d" % j
                qlo = "qkf%d_lo" % j
                P.op("dve", lambda v, j=j, x1v=x1v, cs=cs: v.tensor_tensor(out=rt[j][:, 0], in0=x1v, in1=cs, op=ALU.mult),
                     reads=[qlo, "cosP"], writes=[rtr + "a"])
                P.op("dve", lambda v, j=j, x2v=x2v, sn=sn: v.tensor_tensor(out=rt[j][:, 1], in0=x2v, in1=sn, op=ALU.mult),
                     reads=[qlo, "sinP"], writes=[rtr + "b"])
                P.op("dve", lambda v, j=j, x2v=x2v, cs=cs: v.tensor_tensor(out=rt[j][:, 2], in0=x2v, in1=cs, op=ALU.mult),
                     reads=[qlo], writes=[rtr + "c"])
                P.op("dve", lambda v, j=j, x1v=x1v, sn=sn: v.tensor_tensor(out=rt[j][:, 3], in0=x1v, in1=sn, op=ALU.mult),
                     reads=[qlo], writes=[rtr + "d"])
                P.op("dve", lambda v, j=j: v.tensor_tensor(out=qkf[j][:, :, 0:16], in0=rt[j][:, 0], in1=rt[j][:, 1], op=ALU.subtract),
                     reads=[rtr + "a", rtr + "b", rtr + "c", rtr + "d"], writes=["qkf%d_lo" % j])
                P.op("dve", lambda v, j=j: v.tensor_tensor(out=qkf[j][:, :, 16:32], in0=rt[j][:, 2], in1=rt[j][:, 3], op=ALU.add),
                     reads=[rtr + "c", rtr + "d"], writes=["qkf%d_lo2" % j])
                P.dma("sp", lambda s, j=j, tt=tt, h=h: s.dma_start(out=kp[tt * 128:(tt + 1) * 128, h * 128:(h + 1) * 128], in_=qkf[j][:, 1, :]),
                      reads=["qkf%d_lo" % j, "qkf%d_lo2" % j, "qkf%d_hi" % j])
                P.dma("sp", lambda s, j=j, tt=tt, h=h: s.dma_start(out=vp[tt * 128:(tt + 1) * 128, h * 128:(h + 1) * 128], in_=vf[j][:]),
                      reads=["vf%d" % j])
                P.op("pool", lambda g, j=j: g.tensor_copy(out=qkb[j][:], in_=qkf[j][:]),
                     reads=["qkf%d_lo" % j, "qkf%d_lo2" % j, "qkf%d_hi" % j], writes=["qkb%d" % j])
                if tt >= 2:
                    emit_tr(tt - 2)
                yield
            emit_tr(14)
            emit_tr(15)
            yield
            P.op("dve", lambda v: v.tensor_reduce(out=kmT[:], in_=kT_.rearrange("p (n t) -> p n t", n=8), axis=AX.X, op=ALU.add),
                 reads=[rk], writes=["kmT"])
            P.op("dve", lambda v: v.tensor_scalar(out=kmTb[:], in0=kmT[:], scalar1=1.0 / 256.0, scalar2=None, op0=ALU.mult),
                 reads=["kmT"], writes=["kmTb"])
            for j in range(8):
                tt = 8 + j
                P.op("pe", lambda pe, j=j, tt=tt: pe.matmul(out=pM[:, j * 8:(j + 1) * 8], lhsT=qT_[:, tt * 128:(tt + 1) * 128], rhs=kmTb[:],
                                                            start=True, stop=True), reads=[rq, "kmTb"], writes=[pMn])
            P.op("dve", lambda v: v.tensor_tensor(out=gm[:], in0=pM[:, 0:64].rearrange("p (a b) -> p a b", a=8), in1=candb[:], op=ALU.add),
                 reads=[pMn, "candb"], writes=["gm"])
            yield

            def bc8(t):
                return t[:, :].unsqueeze(2).to_broadcast([128, 8, 8])
            P.op("dve", lambda v: v.tensor_reduce(out=mx1[:], in_=gm[:], axis=AX.X, op=ALU.max), reads=["gm"], writes=["mx1"])
            P.op("dve", lambda v: v.tensor_tensor(out=eqt[:], in0=gm[:], in1=bc8(mx1), op=ALU.is_ge), reads=["gm", "mx1"], writes=["eqt"])
            P.op("dve", lambda v: v.scalar_tensor_tensor(out=g2[:], in0=eqt[:], scalar=-BIG, in1=gm[:], op0=ALU.mult, op1=ALU.add),
                 reads=["eqt", "gm"], writes=["g2"])
            P.op("dve", lambda v: v.tensor_reduce(out=mx1[:], in_=g2[:], axis=AX.X, op=ALU.max), reads=["g2"], writes=["mx1"])
            P.op("dve", lambda v: v.tensor_tensor(out=eqt[:], in0=g2[:], in1=bc8(mx1), op=ALU.is_ge), reads=["g2", "mx1"], writes=["eqt"])
            yield
            P.op("dve", lambda v: v.scalar_tensor_tensor(out=g2[:], in0=eqt[:], scalar=-BIG, in1=g2[:], op0=ALU.mult, op1=ALU.add),
                 reads=["eqt"], writes=["g2"])
            P.op("dve", lambda v: v.tensor_reduce(out=mx1[:], in_=g2[:], axis=AX.X, op=ALU.max), reads=["g2"], writes=["mx1"])
            P.op("dve", lambda v: v.tensor_tensor(out=eqt[:], in0=gm[:], in1=bc8(mx1), op=ALU.is_ge), reads=["gm", "mx1"], writes=["eqt"])
            P.op("dve", lambda v: v.tensor_scalar(out=negb_[:, 8:16, :], in0=eqt[:], scalar1=-NEG, scalar2=NEG, op0=ALU.mult, op1=ALU.add),
                 reads=["eqt"], writes=[rnb])
            yield
            for hf in range(2):
                for t8 in range(8):
                    tt = hf * 8 + t8
                    P.op("pe", lambda pe, tt=tt, t8=t8: pe.transpose(out=pbt[0:8, t8 * 128:(t8 + 1) * 128], in_=negb_[:, tt, :],
                                                                     identity=C.identb[:]),
                         reads=["negb_s", "negb_s1", rnb, "identb"], writes=["pbt"])
                P.op("act", lambda a, hf=hf: a.activation(out=negbT_[:, hf * 1024:(hf + 1) * 1024], in_=pbt[0:8, :], func=AF.Copy),
                     reads=["pbt"], writes=[rnbT])
                yield

        def attn_steps(h):
            par = h % 2
            szT_, qT_, kT_, vb_, negbT_ = szTs[par], qTs[par], kTs[par], vbs[par], negbTs[par]
            rq, rk, rv, rz, rnbT = "qT%d" % par, "kT%d" % par, "vb%d" % par, szTn[par], "negbT%d" % par
            rounds = [(qb, n) for qb in range(8) for n in range(qb + 1)]

            def emit_qk(r):
                qb, n = rounds[r]
                bank, bn = pS[r % 2], pSn[r % 2]
                qc = qb * 256
                sv = bank[:, :].rearrange("p (a q) -> p a q", a=2)
                for a in range(2):
                    kt = 2 * n + a
                    lk = kT_[:, kt * 128:(kt + 1) * 128]
                    if n < qb:
                        P.op("pe", lambda pe, a=a, lk=lk, qc=qc, sv=sv: pe.matmul(out=sv[:, a, :], lhsT=lk, rhs=qT_[:, qc:qc + 256],
                                                                                 start=True, stop=False),
                             reads=[rk, rq], writes=[bn])
                        P.op("pe", lambda pe, a=a, n=n, qc=qc, sv=sv: pe.matmul(out=sv[:, a, :], lhsT=z8[:, n, :], rhs=negbT_[:, qc:qc + 256],
                                                                                start=False, stop=True),
                             reads=["z8", rnbT], writes=[bn])
                    else:
                        if a == 0:
                            P.op("pe", lambda pe, lk=lk, qc=qc, sv=sv: pe.matmul(out=sv[:, 0, 0:128], lhsT=lk, rhs=qT_[:, qc:qc + 128],
                                                                                 start=True, stop=False), reads=[rk, rq], writes=[bn])
                            P.op("pe", lambda pe, sv=sv: pe.matmul(out=sv[:, 0, 0:128], lhsT=C.identb[:], rhs=trib[:], start=False, stop=True),
                                 reads=["identb", "trib"], writes=[bn])
                            P.op("pe", lambda pe, lk=lk, qc=qc, sv=sv: pe.matmul(out=sv[:, 0, 128:256], lhsT=lk, rhs=qT_[:, qc + 128:qc + 256],
                                                                                 start=True, stop=True), reads=[rk, rq], writes=[bn])
                        else:
                            P.op("pe", lambda pe, lk=lk, qc=qc, sv=sv: pe.matmul(out=sv[:, 1, 128:256], lhsT=lk, rhs=qT_[:, qc + 128:qc + 256],
                                                                                 start=True, stop=False), reads=[rk, rq], writes=[bn])
                            P.op("pe", lambda pe, sv=sv: pe.matmul(out=sv[:, 1, 128:256], lhsT=C.identb[:], rhs=trib[:], start=False, stop=True),
                                 reads=["identb", "trib"], writes=[bn])

            def emit_exp(r):
                qb, n = rounds[r]
                bank, bn = pS[r % 2], pSn[r % 2]
                pt_, ptn = pT[r % 3], "pT%d" % (r % 3)
                sv = bank[:, :].rearrange("p (a q) -> p a q", a=2)
                if n < qb:
                    P.op("act", lambda a: a.activation(out=pt_[:], in_=sv, func=AF.Exp, scale=SCALE), reads=[bn], writes=[ptn])
                else:
                    P.op("act", lambda a: a.activation(out=pt_[:, 0, :], in_=sv[:, 0, :], func=AF.Exp, scale=SCALE), reads=[bn], writes=[ptn])
                    P.op("act", lambda a: a.activation(out=pt_[:, 1, 128:256], in_=sv[:, 1, 128:256], func=AF.Exp, scale=SCALE),
                         reads=[bn], writes=[ptn])

            def emit_pv(r):
                qb, n = rounds[r]
                pt_, ptn = pT[r % 3], "pT%d" % (r % 3)
                for a in range(2):
                    kt = 2 * n + a
                    own1 = (n == qb and a == 1)
                    q0 = 128 if own1 else 0
                    first = (n == 0 and a == 0)
                    last = (n == qb and a == 1)
                    P.op("pe", lambda pe, a=a, kt=kt, q0=q0, first=first, last=last: pe.matmul(
                        out=pO[:, q0:256], lhsT=vb_[:, kt, :], rhs=pt_[:, a, q0:256], start=first, stop=last),
                        reads=[rv, ptn], writes=[pOn])
                    P.op("pe", lambda pe, a=a, q0=q0, first=first, last=last: pe.matmul(
                        out=pL[:, q0:256], lhsT=onesb[:], rhs=pt_[:, a, q0:256], start=first, stop=last),
                        reads=["onesb", ptn], writes=[pLn])
                if n == qb:
                    qc = qb * 256
                    P.op("dve", lambda v: v.reciprocal(out=rl[:], in_=pL[:, 0:256]), reads=[pLn], writes=["rl"])
                    P.op("dve", lambda v: v.tensor_tensor(out=otmp[:], in0=pO[:, 0:256], in1=rl[:], op=ALU.mult),
                         reads=[pOn, "rl"], writes=["otmp"])
                    P.op("dve", lambda v, qc=qc: v.tensor_tensor(out=gT[:, h, qc:qc + 256], in0=otmp[:], in1=szT_[:, qc:qc + 256], op=ALU.mult),
                         reads=["otmp", rz], writes=["gT"])

            nr = len(rounds)
            emit_qk(0)
            emit_exp(0)
            for r in range(nr):
                if r + 1 < nr:
                    emit_qk(r + 1)
                    emit_exp(r + 1)
                emit_pv(r)
                yield

        for _ in proj_steps(0):
            pass
        for h in range(NH):
            ag = attn_steps(h)
            pg_ = proj_steps(h + 1) if h + 1 < NH else iter(())
            pdone = False
            k = 0
            for _ in ag:
                k += 1
                if not pdone and k % 4 != 0:
                    try:
                        next(pg_)
                    except StopIteration:
                        pdone = True
            for _ in pg_:
                pass

        C.load_wout(w_out1)
        C.load_ln(lng[1:2, :], lnb[1:2, :])
        def xres1(tt):
            i = (base1 + tt) % C.NB
            P.dma("sp", lambda s: s.dma_start(out=C.xt_ring[i][:], in_=x1d[tt * 128:(tt + 1) * 128, :]),
                  reads=["x1d_%d" % tt], writes=["xt%d" % i])

        base1 = C.tctr
        xres1(0)
        xres1(1)
        for tt in range(16):
            def outf(i, tt=tt):
                P.dma("sp", lambda s: s.dma_start(out=yp[tt * 128:(tt + 1) * 128, :], in_=C.x1t[i][:]), reads=["x1t%d" % i])
            bsel = 2 + (tt % 2) * 2
            nxt = C.outproj_ln(128, lambda fc, tt=tt: gT[:, fc, tt * 128:(tt + 1) * 128], "gT", None, outf,
                               [pb[bsel], pb[bsel + 1]], pbn[bsel:bsel + 2], defer=True)
            if tt + 2 < 16:
                xres1(tt + 2)
            nxt()
        P.dma("sp", lambda s: s.dma_start(out=cst[0:NS, :], in_=gsm), reads=["x1t3"], writes=["x1t3"])
        C.transpose_to(cst, NS, gsT, 0, "x1t3", "gsT", [pb[0], pb[1]], pbn[0:2])

        def xres_s(i):
            P.dma("sp", lambda s: s.dma_start(out=C.xt_ring[i][0:NS, :], in_=xs1m), writes=["xt%d" % i])

        def outf_s(i):
            P.dma("sp", lambda s: s.dma_start(out=ys, in_=C.x1t[i][0:NS, :]), reads=["x1t%d" % i])
        C.outproj_ln(NS, lambda fc: gsT[:, fc, :], "gsT", xres_s, outf_s, [pb[4], pb[5]], pbn[4:6])
        P.emit()
    return nc


def _rope_tables(pos):
    half = 16 // 1
    inv = (np.float32(500000.0) ** (-np.arange(16, dtype=np.float32) * np.float32(2.0) / np.float32(32.0))).astype(np.float32)
    ang = pos.astype(np.float32)[:, None] * inv[None, :]
    return np.cos(ang).astype(np.float32), np.sin(ang).astype(np.float32)


def kernel(x_prompt, x_sample, state_conv, cache_k, cache_v, page_table,
           w_in_conv, w_conv, w_out_conv, w_in_attn, w_out_attn, ln_g, ln_b):
    f = lambda a: np.ascontiguousarray(np.asarray(a, dtype=np.float32))
    x_prompt, x_sample, state_conv = f(x_prompt), f(x_sample), f(state_conv)
    w_in_conv, w_conv, w_out_conv = f(w_in_conv), f(w_conv), f(w_out_conv)
    w_in_attn, w_out_attn, ln_g, ln_b = f(w_in_attn), f(w_out_attn), f(ln_g), f(ln_b)
    cache_k = np.asarray(cache_k)
    cache_v = np.asarray(cache_v)
    pt = np.ascontiguousarray(np.asarray(page_table, dtype=np.int32))
    w_in0_blk = _block_w(w_in_conv[0])
    w_in1_blk = _block_w(w_in_attn[0])
    cos_p, sin_p = _rope_tables(np.arange(T))
    cos_s, sin_s = _rope_tables(np.array([2048]))

    xs = np.ascontiguousarray(x_sample[:, 0, :])
    stc = np.ascontiguousarray(state_conv[0])
    in_a = []
    for c in range(8):
        ckc = np.ascontiguousarray(cache_k[0, :, :, c, :], dtype=np.float32).reshape(NPHYS * NQ, CH)
        cvc = np.ascontiguousarray(cache_v[0, :, :, c, :], dtype=np.float32).reshape(NPHYS * NQ, CH)
        w1c = w_in1_blk[c]
        in_a.append({"xs": xs, "stc": stc, "ptab": pt, "ck": ckc, "cv": cvc, "w_in0": w_in0_blk, "wconv": w_conv[0],
                     "w_out0": w_out_conv[0], "w1c": w1c, "lng0": ln_g[0:1], "lnb0": ln_b[0:1], "coss": cos_s, "sins": sin_s})
    ra = run_bass_kernel_spmd(build_A(), in_a, core_ids=list(range(8))).results
    conv_state_sample = ra[0]["css"][None]
    xs1 = ra[0]["xs1o"]
    k_sample = np.stack([ra[c]["ksm"] for c in range(8)], axis=1)[None, :, None]
    v_sample = np.stack([ra[c]["vsm"] for c in range(8)], axis=1)[None, :, None]
    g_all = np.concatenate([ra[c]["gco"] for c in range(8)], axis=1)

    in_b = []
    for c in range(8):
        in_b.append({"xp": x_prompt[c], "gsm": np.ascontiguousarray(g_all[16 * c:16 * c + 16]),
                     "xs1m": np.ascontiguousarray(xs1[16 * c:16 * c + 16]),
                     "w_in0": w_in0_blk, "wconv": w_conv[0], "w_out0": w_out_conv[0], "w_in1": w_in1_blk,
                     "w_out1": w_out_attn[0], "lng": ln_g, "lnb": ln_b, "cosp": cos_p, "sinp": sin_p})
    rb = run_bass_kernel_spmd(build_B(), in_b, core_ids=list(range(8))).results
    y_prompt = np.stack([rb[c]["yp"] for c in range(8)], axis=0)
    y_sample = np.concatenate([rb[c]["ys"] for c in range(8)], axis=0)[:, None, :]
    conv_state_prompt = np.stack([rb[c]["csp"] for c in range(8)], axis=0)[None]
    k_prompt = np.stack([rb[c]["kp"] for c in range(8)], axis=0).reshape(1, 8, 16, 128, 8, 128)
    v_prompt = np.stack([rb[c]["vp"] for c in range(8)], axis=0).reshape(1, 8, 16, 128, 8, 128)
    return (y_prompt.astype(np.float32), y_sample.astype(np.float32), conv_state_prompt.astype(np.float32),
            conv_state_sample.astype(np.float32), k_prompt.astype(np.float32), v_prompt.astype(np.float32),
            k_sample.astype(np.float32), v_sample.astype(np.float32))
```

```python
import contextlib
import numpy as np
import concourse.bass as bass
import concourse.mybir as mybir
from concourse.bass_utils import run_bass_kernel_spmd

F32 = mybir.dt.float32
BF16 = mybir.dt.bfloat16
I32 = mybir.dt.int32
AF = mybir.ActivationFunctionType
ALU = mybir.AluOpType
AX = mybir.AxisListType

D = 1024
T = 2048
NSA = 128
NS = 16
NPG = 16
NPHYS = 2560
HD = 128
NH = 8
NQ = 4
CH = 128 * HD // NQ
TPC = 128 // NQ
ALPHA = float((2.0 * 2) ** 0.25)
EPS = 1e-5
SCALE = float(HD ** -0.5)
NEG = -30000.0
BIG = 1.0e30


class Op:
    __slots__ = ("e", "fn", "deps", "dma", "sig", "cnt", "sem")

    def __init__(self, e, fn, deps, dma):
        self.e, self.fn, self.deps, self.dma = e, fn, deps, dma
        self.sig = False
        self.cnt = None
        self.sem = None


class Res:
    __slots__ = ("w", "r")

    def __init__(self):
        self.w = []
        self.r = []


def _push(lst, o):
    if not o.dma:
        for i, x in enumerate(lst):
            if (not x.dma) and x.e == o.e:
                lst[i] = o
                return
    lst.append(o)


class Prog:
    def __init__(self, nc, n_dma_sems=12):
        self.nc = nc
        self.engs = {"pe": nc.tensor, "act": nc.scalar, "dve": nc.vector, "pool": nc.gpsimd, "sp": nc.sync}
        self.ops = {e: [] for e in self.engs}
        self.n_dma_sems = n_dma_sems
        self.R = {}

    def res(self, name):
        if name not in self.R:
            self.R[name] = Res()
        return self.R[name]

    def _mk(self, e, fn, reads, writes, deps, dma):
        reads = [self.res(r) if isinstance(r, str) else r for r in reads]
        writes = [self.res(w) if isinstance(w, str) else w for w in writes]
        dl = [d for d in deps if d is not None]
        for r in reads:
            dl.extend(r.w)
        for w in writes:
            dl.extend(w.w)
            dl.extend(w.r)
        if e == "pe":
            dl = [d for d in dl if d.e != "pe" or d.dma]
        o = Op(e, fn, dl, dma)
        for r in reads:
            _push(r.r, o)
        for w in writes:
            w.w = [o]
            w.r = []
        self.ops[e].append(o)
        return o

    def op(self, e, fn, reads=(), writes=(), deps=()):
        return self._mk(e, fn, reads, writes, deps, False)

    def dma(self, e, fn, reads=(), writes=(), deps=()):
        return self._mk(e, fn, reads, writes, deps, True)

    def emit(self):
        nc = self.nc
        for e, lst in self.ops.items():
            for o in lst:
                for d in o.deps:
                    d.sig = True
        with contextlib.ExitStack() as st:
            csem = {e: st.enter_context(nc.semaphore("c_" + e)) for e in self.engs}
            dsem = {e: [st.enter_context(nc.semaphore("d_%s%d" % (e, i))) for i in range(self.n_dma_sems)]
                    for e in ("sp", "act", "pool")}
            for e, lst in self.ops.items():
                c = 0
                dcnt = [0] * self.n_dma_sems
                prev = [None] * self.n_dma_sems
                k = 0
                for o in lst:
                    if o.dma:
                        j = k % self.n_dma_sems
                        k += 1
                        dcnt[j] += 16
                        o.sem, o.cnt = dsem[e][j], dcnt[j]
                        if prev[j] is not None:
                            o.deps.append(prev[j])
                        prev[j] = o
                    elif o.sig:
                        c += 1
                        o.sem, o.cnt = csem[e], c
            block = st.enter_context(nc.Block())

            def mk(e):
                def body(eng):
                    waited = {}
                    for o in self.ops[e]:
                        for d in o.deps:
                            key = id(d.sem)
                            if waited.get(key, 0) >= d.cnt:
                                continue
                            eng.wait_ge(d.sem, d.cnt)
                            waited[key] = d.cnt
                        ins = o.fn(eng)
                        if o.dma:
                            ins.then_inc(o.sem, 16)
                        elif o.sig:
                            ins.then_inc(o.sem, 1)
                    last = {}
                    for o in self.ops[e]:
                        if o.dma:
                            last[id(o.sem)] = o
                    for o in last.values():
                        if waited.get(id(o.sem), 0) < o.cnt:
                            eng.wait_ge(o.sem, o.cnt)
                return body

            block.tensor(mk("pe"))
            block.scalar(mk("act"))
            block.vector(mk("dve"))
            block.gpsimd(mk("pool"))
            block.sync(mk("sp"))


class Common:
    def __init__(self, nc, P, sb, ps, nb=2):
        self.nc, self.P, self.sb, self.ps = nc, P, sb, ps
        self.ident = sb("ident", [128, 128], F32)
        self.identb = sb("identb", [128, 128], BF16)
        self.wbb = [sb("wbb%d" % i, [128, 8, 512], BF16) for i in range(2)]
        self.wob = sb("wob", [128, 8, D], BF16)
        self.gbt = sb("gbt", [128, D], F32)
        self.bbt = sb("bbt", [128, D], F32)
        NB = nb
        self.NB = NB
        self.xt_ring = [sb("xt%d" % i, [128, D], F32) for i in range(NB)]
        self.x1t = [sb("x1t%d" % i, [128, D], F32) for i in range(NB)]
        self.stats_l = [sb("stats%d" % i, [128, 2, 6], F32) for i in range(NB)]
        self.mv_l = [sb("mv%d" % i, [128, 2], F32) for i in range(NB)]
        self.rstd_l = [sb("rstd%d" % i, [128, 1], F32) for i in range(NB)]
        self.nmr_l = [sb("nmr%d" % i, [128, 1], F32) for i in range(NB)]
        self.wctr = 0
        self.tctr = 0
        ident, identb = self.ident, self.identb
        P.op("pool", lambda g: g.memset(ident[:], 1.0), writes=["ident"])
        P.op("pool", lambda g: g.affine_select(out=ident[:], in_=ident[:], pattern=[[-1, 128]], compare_op=ALU.is_equal,
                                                fill=0.0, base=0, channel_multiplier=1), writes=["ident"])
        P.op("pool", lambda g: g.tensor_copy(out=identb[:], in_=ident[:]), reads=["ident"], writes=["identb"])

    def transpose_to(self, src, n, dst, col0, rsrc, rdst, banks, bnames):
        P, ident = self.P, self.ident
        for half in range(2):
            bank, bn = banks[half], bnames[half]
            for j in range(4):
                kc = half * 4 + j
                P.op("pe", lambda pe, kc=kc, j=j, bank=bank: pe.transpose(
                    out=bank[:, j * 128:j * 128 + n], in_=src[0:n, kc * 128:(kc + 1) * 128], identity=ident[0:n, 0:n]),
                    reads=[rsrc, "ident"], writes=[bn])
            P.op("act", lambda a, half=half, bank=bank: a.activation(
                out=dst[:, half * 4:half * 4 + 4, col0:col0 + n],
                in_=bank[:, :].rearrange("p (j c) -> p j c", j=4)[:, :, 0:n], func=AF.Copy),
                reads=[bn], writes=[rdst])

    def load_w_block(self, src):
        i = self.wctr % 2
        self.wctr += 1
        wbb = self.wbb
        for half in range(2):
            self.P.dma("pool", lambda g, half=half: g.dma_start(
                out=wbb[i][:, half * 4:half * 4 + 4, :], in_=src[:, half * 4:half * 4 + 4, :]), writes=["wbb%d" % i])
        return i

    def load_wout(self, w):
        wob = self.wob
        for half in range(2):
            for kh in range(2):
                self.P.dma("pool", lambda g, half=half, kh=kh: g.dma_start(
                    out=wob[:, kh * 4:kh * 4 + 4, half * 512:(half + 1) * 512],
                    in_=w.rearrange("(kc p) c -> p kc c", p=128)[:, kh * 4:kh * 4 + 4, half * 512:(half + 1) * 512]),
                    writes=["wob"])

    def load_ln(self, lng_row, lnb_row):
        gbt, bbt = self.gbt, self.bbt
        self.P.dma("sp", lambda s: s.dma_start(out=gbt[:], in_=lng_row.partition_broadcast(128)), writes=["gbt"])
        self.P.dma("sp", lambda s: s.dma_start(out=bbt[:], in_=lnb_row.partition_broadcast(128)), writes=["bbt"])

    def outproj_ln(self, n, lhs_fn, rlhs, xres_fn, out_fn, banks, bnames, defer=False):
        P = self.P
        i = self.tctr % self.NB
        self.tctr += 1
        wob, xt_ring, x1t = self.wob, self.xt_ring, self.x1t
        stats, mv, rstd, nmr, gbt, bbt = self.stats_l[i], self.mv_l[i], self.rstd_l[i], self.nmr_l[i], self.gbt, self.bbt
        rs_, rm_, rr_, rn_ = "stats%d" % i, "mv%d" % i, "rstd%d" % i, "nmr%d" % i
        for half in range(2):
            for fc in range(8):
                P.op("pe", lambda pe, half=half, fc=fc: pe.matmul(
                    out=banks[half][0:n, :], lhsT=lhs_fn(fc), rhs=wob[:, fc, half * 512:(half + 1) * 512],
                    start=(fc == 0), stop=(fc == 7)),
                    reads=[rlhs, "wob"], writes=[bnames[half]])
        if xres_fn is not None:
            xres_fn(i)
        for half in range(2):
            P.op("dve", lambda v, half=half: v.scalar_tensor_tensor(
                out=x1t[i][0:n, half * 512:(half + 1) * 512], in0=xt_ring[i][0:n, half * 512:(half + 1) * 512],
                scalar=ALPHA, in1=banks[half][0:n, :], op0=ALU.mult, op1=ALU.add),
                reads=["xt%d" % i, bnames[half]], writes=["x1t%d_h%d" % (i, half)], deps=self.P.res("x1t%d" % i).r + self.P.res("x1t%d" % i).w)
        for half in range(2):
            P.op("dve", lambda v, half=half: v.bn_stats(out=stats[0:n, half, :], in_=x1t[i][0:n, half * 512:(half + 1) * 512]),
                 reads=["x1t%d_h%d" % (i, half)], writes=[rs_ + "_%d" % half])
        P.op("dve", lambda v: v.bn_aggr(out=mv[0:n, :], in_=stats[0:n, :, :].rearrange("p a b -> p (a b)")),
             reads=[rs_ + "_0", rs_ + "_1"], writes=[rm_])
        P.op("act", lambda a: a.activation(out=rstd[0:n, :], in_=mv[0:n, 1:2], func=AF.Sqrt, bias=EPS, scale=1.0),
             reads=[rm_], writes=[rr_])
        P.op("dve", lambda v: v.reciprocal(out=rstd[0:n, :], in_=rstd[0:n, :]), reads=[rr_], writes=[rr_])
        P.op("dve", lambda v: v.scalar_tensor_tensor(out=nmr[0:n, :], in0=mv[0:n, 0:1], scalar=-1.0, in1=rstd[0:n, :],
                                                     op0=ALU.mult, op1=ALU.mult),
             reads=[rm_, rr_], writes=[rn_])
        P.op("act", lambda a: a.activation(out=x1t[i][0:n, :], in_=x1t[i][0:n, :], func=AF.Identity,
                                           bias=nmr[0:n, :], scale=rstd[0:n, :]),
             reads=[rn_, rr_, "x1t%d_h0" % i, "x1t%d_h1" % i, rs_ + "_0", rs_ + "_1"], writes=["x1t%d" % i, "x1t%d_h0" % i, "x1t%d_h1" % i])
        P.op("pool", lambda v: v.tensor_tensor(out=x1t[i][0:n, :], in0=x1t[i][0:n, :], in1=gbt[0:n, :], op=ALU.mult),
             reads=["gbt"], writes=["x1t%d" % i])
        P.op("pool", lambda v: v.tensor_tensor(out=x1t[i][0:n, :], in0=x1t[i][0:n, :], in1=bbt[0:n, :], op=ALU.add),
             reads=["bbt"], writes=["x1t%d" % i])
        if defer:
            return lambda: out_fn(i)
        out_fn(i)


def wblock_view(w, blk):
    return w[blk].rearrange("(kc p) c -> p kc c", p=128)


def _block_w(w):
    return np.ascontiguousarray(w.reshape(D, 4, 8, HD).transpose(2, 0, 1, 3).reshape(8, D, 4 * HD))


def build_A():
    nc = bass.Bass("TRN2", target_bir_lowering=False)

    def din(name, shape, dt=F32):
        return nc.dram_tensor(name, shape, dt, kind="ExternalInput").ap()

    def dout(name, shape, dt=F32):
        return nc.dram_tensor(name, shape, dt, kind="ExternalOutput").ap()

    xs = din("xs", [NSA, D])
    stc = din("stc", [NSA, 2, D])
    ptab = din("ptab", [NSA, NPG], I32)
    ck = din("ck", [NPHYS * NQ, CH])
    cv = din("cv", [NPHYS * NQ, CH])
    w_in0 = din("w_in0", [8, D, 512])
    wconv = din("wconv", [3, D])
    w_out0 = din("w_out0", [D, D])
    w1c = din("w1c", [D, 512])
    lng0 = din("lng0", [1, D])
    lnb0 = din("lnb0", [1, D])
    coss = din("coss", [1, 16])
    sins = din("sins", [1, 16])

    css = dout("css", [NSA, 2, D])
    xs1o = dout("xs1o", [NSA, D])
    ksm = dout("ksm", [NSA, HD])
    vsm = dout("vsm", [NSA, HD])
    gco = dout("gco", [NSA, HD])

    with contextlib.ExitStack() as st:
        def sb(name, shape, dt=F32):
            return st.enter_context(nc.sbuf_tensor(name, shape, dt))

        def ps(name, shape, dt=F32):
            return st.enter_context(nc.psum_tensor(name, shape, dt))

        P = Prog(nc)
        C = Common(nc, P, sb, ps, nb=1)
        pb = [ps("pb%d" % i, [128, 512], F32) for i in range(8)]
        pbn = ["pb%d" % i for i in range(8)]

        xsT = sb("xsT", [128, 8, NSA], BF16)
        gsT = sb("gsT", [128, 8, NSA], BF16)
        st_sb = sb("st_sb", [NSA, 2, D], F32)
        wcb = sb("wcb", [NSA, 3, D], F32)
        gs = sb("gs", [NSA, D], F32)
        us = sb("us", [NSA, D], F32)
        pev = sb("pev", [NSA, 512], F32)
        szb = sb("szb", [NSA, 128], F32)
        cvb = sb("cvb", [NSA, 128], F32)
        tb = sb("tb", [NSA, 128], F32)
        cosS = sb("cosS", [NSA, 16], F32)
        sinS = sb("sinS", [NSA, 16], F32)
        qk = sb("qk", [NSA, 2, HD], F32)
        vn = sb("vn", [NSA, HD], F32)
        szs = sb("szs", [NSA, HD], F32)
        rts = sb("rts", [NSA, 4, 2, 16], F32)
        pti = sb("pti", [NSA, NPG], I32)
        ptf = sb("ptf", [NSA, NPG], F32)
        idxk = sb("idxk", [NSA, NPG, NQ], I32)
        idxkf = sb("idxkf", [NSA, NPG, NQ], F32)
        NKB = 4
        kch = [sb("kch%d" % i, [NSA, CH], F32) for i in range(NKB)]
        kcbig = sb("kcbig", [NSA, 2 * CH], BF16)
        kcb = [kcbig[:, 0:CH], kcbig[:, CH:2 * CH]]
        vch = kch
        prodf = sb("prodf", [NSA, CH], F32)
        prod = prodf[:, 0:8 * HD]
        prodb = kcb
        prodf2 = kcbig[:, :].bitcast(F32)
        prodv = kcb
        part = sb("part", [NSA, HD], F32)
        kmacc = sb("kmacc", [NSA, 8, HD], F32)
        gate = sb("gate", [NSA, 8], F32)
        g2 = sb("g2", [NSA, 8], F32)
        eqt = sb("eqt", [NSA, 8], F32)
        sel = sb("sel", [NSA, 8], F32)
        mx1 = sb("mx1", [NSA, 1], F32)
        cnt = sb("cnt", [NSA, 8], F32)
        oh = sb("oh", [NSA, 8], F32)
        ptmp = sb("ptmp", [NSA, 2, 8], F32)
        physf = sb("physf", [NSA, 3, 2], F32)
        idxsf = sb("idxsf", [NSA, 6, NQ], F32)
        idxs = sb("idxs", [NSA, 6, NQ], I32)
        sc = sb("sc", [NSA, 6 * 128 + 1], F32)
        pp = sb("pp", [NSA, 6 * 128 + 1], F32)
        smx = sb("smx", [NSA, 1], F32)
        lsum = sb("lsum", [NSA, 1], F32)
        oacc = sb("oacc", [NSA, HD], F32)

        P.dma("sp", lambda s: s.dma_start(out=C.xt_ring[0][:], in_=xs), writes=["xt0"])
        P.dma("sp", lambda s: s.dma_start(out=st_sb[:], in_=stc), writes=["st_sb"])
        for j in range(3):
            P.dma("sp", lambda s, j=j: s.dma_start(out=wcb[:, j, :], in_=wconv[j:j + 1, :].partition_broadcast(NSA)), writes=["wcb"])
        P.dma("sp", lambda s: s.dma_start(out=cosS[:], in_=coss.partition_broadcast(NSA)), writes=["cosS"])
        P.dma("sp", lambda s: s.dma_start(out=sinS[:], in_=sins.partition_broadcast(NSA)), writes=["sinS"])
        P.dma("sp", lambda s: s.dma_start(out=pti[:], in_=ptab), writes=["pti"])
        P.dma("sp", lambda s: s.dma_start(out=css[:, 0, :], in_=st_sb[:, 1, :]), reads=["st_sb"])

        P.op("dve", lambda v: v.tensor_copy(out=ptf[:], in_=pti[:]), reads=["pti"], writes=["ptf"])
        for tq in range(NQ):
            P.op("dve", lambda v, tq=tq: v.tensor_scalar(out=idxkf[:, :, tq], in0=ptf[:], scalar1=float(NQ), scalar2=float(tq),
                                                         op0=ALU.mult, op1=ALU.add), reads=["ptf"], writes=["idxkf"])
        P.op("dve", lambda v: v.tensor_copy(out=idxk[:], in_=idxkf[:]), reads=["idxkf"], writes=["idxk"])

        def kstream():
            n_ch = 0
            for pg in range(NPG):
                n = pg // 2
                for tq in range(NQ):
                    b = n_ch % NKB
                    n_ch += 1
                    P.dma("pool", lambda g, b=b, pg=pg, tq=tq: g.indirect_dma_start(
                        out=kch[b][:], out_offset=None, in_=ck,
                        in_offset=bass.IndirectOffsetOnAxis(ap=idxk[:, pg, tq:tq + 1], axis=0)),
                        reads=["idxk"], writes=["kch%d" % b])
                    first = (pg % 2 == 0 and tq == 0)
                    dst = kmacc[:, n, :] if first else part[:]
                    P.op("dve", lambda v, b=b, dst=dst: v.tensor_reduce(
                        out=dst, in_=kch[b][:, :].rearrange("p (t d) -> p d t", d=HD), axis=AX.X, op=ALU.add),
                        reads=["kch%d" % b], writes=["kmacc" if first else "part"])
                    if not first:
                        P.op("dve", lambda v, n=n: v.tensor_tensor(out=kmacc[:, n, :], in0=kmacc[:, n, :], in1=part[:], op=ALU.add),
                             reads=["part"], writes=["kmacc"])
                    yield

        ks = kstream()

        def pump(k):
            for _ in range(k):
                try:
                    next(ks)
                except StopIteration:
                    return

        import os
        STOP = int(os.environ.get('K_STOP', '99'))
        pump(5)
        if STOP <= 0:
            pump(1000); P.emit(); return nc
        C.transpose_to(C.xt_ring[0], NSA, xsT, 0, "xt0", "xsT", [pb[0], pb[1]], pbn[0:2])
        for fc in range(8):
            wi = C.load_w_block(wblock_view(w_in0, fc))
            bank, bn = pb[2 + fc % 2], pbn[2 + fc % 2]
            for kc in range(8):
                P.op("pe", lambda pe, kc=kc, wi=wi, bank=bank: pe.matmul(out=bank[:, :], lhsT=xsT[:, kc, :], rhs=C.wbb[wi][:, kc, :],
                                                                         start=(kc == 0), stop=(kc == 7)),
                     reads=["xsT", "wbb%d" % wi], writes=[bn])
            cols = slice(fc * 128, (fc + 1) * 128)
            P.op("act", lambda a, bank=bank: a.activation(out=pev[:], in_=bank[:, :], func=AF.Copy), reads=[bn], writes=["pev"])
            pB, pC, pH, pZ = pev[:, 0:128], pev[:, 128:256], pev[:, 256:384], pev[:, 384:512]
            P.op("act", lambda a, pZ=pZ: a.activation(out=szb[:], in_=pZ, func=AF.Silu), reads=["pev"], writes=["szb"])
            P.op("dve", lambda v, pC=pC, pH=pH, cols=cols: v.tensor_tensor(out=us[:, cols], in0=pC, in1=pH, op=ALU.mult),
                 reads=["pev"], writes=["us"])
            P.op("dve", lambda v, cols=cols: v.tensor_tensor(out=cvb[:], in0=us[:, cols], in1=wcb[:, 2, cols], op=ALU.mult),
                 reads=["us", "wcb"], writes=["cvb"])
            P.op("dve", lambda v, cols=cols: v.tensor_tensor(out=tb[:], in0=st_sb[:, 1, cols], in1=wcb[:, 1, cols], op=ALU.mult),
                 reads=["st_sb", "wcb"], writes=["tb"])
            P.op("dve", lambda v: v.tensor_tensor(out=cvb[:], in0=cvb[:], in1=tb[:], op=ALU.add), reads=["tb"], writes=["cvb"])
            P.op("dve", lambda v, cols=cols: v.tensor_tensor(out=tb[:], in0=st_sb[:, 0, cols], in1=wcb[:, 0, cols], op=ALU.mult),
                 reads=["st_sb", "wcb"], writes=["tb"])
            P.op("dve", lambda v: v.tensor_tensor(out=cvb[:], in0=cvb[:], in1=tb[:], op=ALU.add), reads=["tb"], writes=["cvb"])
            P.op("dve", lambda v: v.tensor_tensor(out=cvb[:], in0=cvb[:], in1=szb[:], op=ALU.mult), reads=["szb"], writes=["cvb"])
            P.op("dve", lambda v, pB=pB, cols=cols: v.tensor_tensor(out=gs[:, cols], in0=pB, in1=cvb[:], op=ALU.mult),
                 reads=["pev", "cvb"], writes=["gs"])
            pump(3)
        P.dma("sp", lambda s: s.dma_start(out=css[:, 1, :], in_=us[:]), reads=["us"])
        C.transpose_to(gs, NSA, gsT, 0, "gs", "gsT", [pb[0], pb[1]], pbn[0:2])
        C.load_wout(w_out0)
        C.load_ln(lng0, lnb0)

        def xres(i):
            P.dma("sp", lambda s: s.dma_start(out=C.xt_ring[i][:], in_=xs), writes=["xt%d" % i])

        xi = [0]

        def outf(i):
            xi[0] = i
            P.dma("sp", lambda s: s.dma_start(out=xs1o, in_=C.x1t[i][:]), reads=["x1t%d" % i])
            C.transpose_to(C.x1t[i], NSA, xsT, 0, "x1t%d" % i, "xsT", [pb[0], pb[1]], pbn[0:2])
        C.outproj_ln(NSA, lambda fc: gsT[:, fc, :], "gsT", xres, outf, [pb[4], pb[5]], pbn[4:6])
        pump(6)
        if STOP <= 1:
            pump(1000); P.emit(); return nc

        wi = C.wctr % 2
        C.wctr += 1
        for half in range(2):
            P.dma("pool", lambda g, half=half: g.dma_start(
                out=C.wbb[wi][:, half * 4:half * 4 + 4, :],
                in_=w1c.rearrange("(kc p) c -> p kc c", p=128)[:, half * 4:half * 4 + 4, :]), writes=["wbb%d" % wi])
        SUB = int(os.environ.get("K_SUB", "99"))
        if SUB <= 1:
            pump(1000); P.emit(); return nc
        bank, bn = pb[2], pbn[2]
        for kc in range(8):
            P.op("pe", lambda pe, kc=kc: pe.matmul(out=bank[:, :], lhsT=xsT[:, kc, :], rhs=C.wbb[wi][:, kc, :],
                                                   start=(kc == 0), stop=(kc == 7)),
                 reads=["xsT", "wbb%d" % wi], writes=[bn])
        if SUB <= 2:
            pump(1000); P.emit(); return nc
        qk_ps = bank[:, 0:256].rearrange("p (a c) -> p a c", a=2)
        qraw = sb("qraw", [NSA, 2, HD], F32)
        P.op("act", lambda a: a.activation(out=qraw[:], in_=qk_ps, func=AF.Copy), reads=[bn], writes=["qraw"])
        P.op("act", lambda a: a.activation(out=qk[:, :, 32:128], in_=qk_ps[:, :, 32:128], func=AF.Copy), reads=[bn], writes=["qk_hi"])
        P.op("act", lambda a: a.activation(out=vn[:], in_=bank[:, 256:384], func=AF.Copy), reads=[bn], writes=["vn"])
        P.op("act", lambda a: a.activation(out=szs[:], in_=bank[:, 384:512], func=AF.Silu), reads=[bn], writes=["szs"])
        if SUB <= 3:
            pump(1000); P.emit(); return nc
        cs = cosS[:, :].unsqueeze(1).to_broadcast([NSA, 2, 16])
        sn = sinS[:, :].unsqueeze(1).to_broadcast([NSA, 2, 16])
        x1v, x2v = qraw[:, :, 0:16], qraw[:, :, 16:32]
        P.op("dve", lambda v: v.tensor_tensor(out=rts[:, 0], in0=x1v, in1=cs, op=ALU.mult), reads=["qraw", "cosS"], writes=["rts"])
        P.op("dve", lambda v: v.tensor_tensor(out=rts[:, 1], in0=x2v, in1=sn, op=ALU.mult), reads=["qraw", "sinS"], writes=["rts"])
        P.op("dve", lambda v: v.tensor_tensor(out=rts[:, 2], in0=x2v, in1=cs, op=ALU.mult), reads=["qraw"], writes=["rts"])
        P.op("dve", lambda v: v.tensor_tensor(out=rts[:, 3], in0=x1v, in1=sn, op=ALU.mult), reads=["qraw"], writes=["rts"])
        P.op("dve", lambda v: v.tensor_tensor(out=qk[:, :, 0:16], in0=rts[:, 0], in1=rts[:, 1], op=ALU.subtract), reads=["rts"], writes=["qk_lo"])
        P.op("dve", lambda v: v.tensor_tensor(out=qk[:, :, 16:32], in0=rts[:, 2], in1=rts[:, 3], op=ALU.add), reads=["rts"], writes=["qk_lo"])
        if not os.environ.get("SKIP_OUT"):
            P.dma("sp", lambda s: s.dma_start(out=ksm, in_=qk[:, 1, :]), reads=["qk_lo", "qk_hi"])
            P.dma("sp", lambda s: s.dma_start(out=vsm, in_=vn[:]), reads=["vn"])
        pump(1000)
        if STOP <= 2:
            P.emit(); return nc

        qb8 = qk[:, 0, :].unsqueeze(1).to_broadcast([NSA, 8, HD])
        P.op("dve", lambda v: v.tensor_tensor(out=prod[:, 0:8 * HD].rearrange("p (n d) -> p n d", n=8), in0=kmacc[:], in1=qb8, op=ALU.mult),
             reads=["kmacc", "qk_lo", "qk_hi"], writes=["prod"])
        P.op("dve", lambda v: v.tensor_reduce(out=gate[:], in_=prod[:, 0:8 * HD].rearrange("p (n d) -> p n d", n=8), axis=AX.X, op=ALU.add),
             reads=["prod"], writes=["gate"])

        def bc(t):
            return t[:, 0:1].to_broadcast([NSA, 8])
        P.op("dve", lambda v: v.tensor_reduce(out=mx1[:], in_=gate[:], axis=AX.X, op=ALU.max), reads=["gate"], writes=["mx1"])
        P.op("dve", lambda v: v.tensor_tensor(out=eqt[:], in0=gate[:], in1=bc(mx1), op=ALU.is_ge), reads=["gate", "mx1"], writes=["eqt"])
        P.op("dve", lambda v: v.scalar_tensor_tensor(out=g2[:], in0=eqt[:], scalar=-BIG, in1=gate[:], op0=ALU.mult, op1=ALU.add),
             reads=["eqt", "gate"], writes=["g2"])
        P.op("dve", lambda v: v.tensor_reduce(out=mx1[:], in_=g2[:], axis=AX.X, op=ALU.max), reads=["g2"], writes=["mx1"])
        P.op("dve", lambda v: v.tensor_tensor(out=eqt[:], in0=g2[:], in1=bc(mx1), op=ALU.is_ge), reads=["g2", "mx1"], writes=["eqt"])
        P.op("dve", lambda v: v.scalar_tensor_tensor(out=g2[:], in0=eqt[:], scalar=-BIG, in1=g2[:], op0=ALU.mult, op1=ALU.add),
             reads=["eqt"], writes=["g2"])
        P.op("dve", lambda v: v.tensor_reduce(out=mx1[:], in_=g2[:], axis=AX.X, op=ALU.max), reads=["g2"], writes=["mx1"])
        P.op("dve", lambda v: v.tensor_tensor(out=sel[:], in0=gate[:], in1=bc(mx1), op=ALU.is_ge), reads=["gate", "mx1"], writes=["sel"])
        P.op("dve", lambda v: v.memset(cnt[:, 0:1], 0.0), writes=["cnt"])
        for n in range(1, 8):
            P.op("dve", lambda v, n=n: v.tensor_tensor(out=cnt[:, n:n + 1], in0=cnt[:, n - 1:n], in1=sel[:, n - 1:n], op=ALU.add),
                 reads=["sel"], writes=["cnt"])
        ptv = ptf[:, :].rearrange("p (n a) -> p a n", a=2)
        for j in range(3):
            P.op("dve", lambda v, j=j: v.tensor_scalar(out=oh[:], in0=cnt[:], scalar1=float(j), scalar2=None, op0=ALU.is_equal),
                 reads=["cnt"], writes=["oh"])
            P.op("dve", lambda v: v.tensor_tensor(out=oh[:], in0=oh[:], in1=sel[:], op=ALU.mult), reads=["sel"], writes=["oh"])
            P.op("dve", lambda v: v.tensor_tensor(out=ptmp[:], in0=ptv, in1=oh[:, :].unsqueeze(1).to_broadcast([NSA, 2, 8]), op=ALU.mult),
                 reads=["oh", "ptf"], writes=["ptmp"])
            P.op("dve", lambda v, j=j: v.tensor_reduce(out=physf[:, j, :], in_=ptmp[:], axis=AX.X, op=ALU.add),
                 reads=["ptmp"], writes=["physf"])
        for tq in range(NQ):
            P.op("dve", lambda v, tq=tq: v.tensor_scalar(out=idxsf[:, :, tq], in0=physf[:, :, :].rearrange("p j a -> p (j a)"),
                                                         scalar1=float(NQ), scalar2=float(tq), op0=ALU.mult, op1=ALU.add),
                 reads=["physf"], writes=["idxsf"])
        P.op("dve", lambda v: v.tensor_copy(out=idxs[:], in_=idxsf[:]), reads=["idxsf"], writes=["idxs"])

        if STOP <= 3:
            P.emit(); return nc
        qbt = qk[:, 0, :].unsqueeze(1).to_broadcast([NSA, TPC, HD])
        NG = CH // 512
        n_ch = 0
        for ja in range(6):
            for tq in range(NQ):
                b = n_ch % NKB
                pbi = n_ch % 2
                n_ch += 1
                P.dma("pool", lambda g, b=b, ja=ja, tq=tq: g.indirect_dma_start(
                    out=kch[b][:], out_offset=None, in_=ck,
                    in_offset=bass.IndirectOffsetOnAxis(ap=idxs[:, ja, tq:tq + 1], axis=0)),
                    reads=["idxs"], writes=["kch%d" % b])
                pf, pfn = (prodf, "prodf") if n_ch % 2 == 0 else (prodf2, "prodf2")
                meng = "pool" if n_ch % 3 == 0 else "dve"
                P.op(meng, lambda v, b=b, pf=pf: v.tensor_tensor(out=pf[:, :].rearrange("p (t d) -> p t d", d=HD),
                                                                in0=kch[b][:, :].rearrange("p (t d) -> p t d", d=HD), in1=qbt, op=ALU.mult),
                     reads=["kch%d" % b, "qk_lo", "qk_hi"], writes=[pfn])
                c0 = ja * 128 + tq * TPC
                P.op("dve", lambda v, c0=c0, pf=pf: v.tensor_reduce(out=sc[:, c0:c0 + TPC], in_=pf[:, :].rearrange("p (t d) -> p t d", d=HD),
                                                                    axis=AX.X, op=ALU.add), reads=[pfn], writes=["sc_%d" % (n_ch % 2)])
        P.op("dve", lambda v: v.tensor_tensor(out=part[:], in0=qk[:, 0, :], in1=qk[:, 1, :], op=ALU.mult),
             reads=["qk_lo", "qk_hi"], writes=["part"])
        P.op("dve", lambda v: v.tensor_reduce(out=sc[:, 768:769], in_=part[:], axis=AX.X, op=ALU.add), reads=["part"], writes=["sc"])
        P.op("dve", lambda v: v.tensor_reduce(out=smx[:], in_=sc[:], axis=AX.X, op=ALU.max), reads=["sc", "sc_0", "sc_1"], writes=["smx"])
        P.op("dve", lambda v: v.tensor_scalar(out=smx[:], in0=smx[:], scalar1=-SCALE, scalar2=None, op0=ALU.mult), writes=["smx"])
        P.op("act", lambda a: a.activation(out=pp[:], in_=sc[:], func=AF.Exp, bias=smx[:], scale=SCALE, accum_out=lsum[:]),
             reads=["sc", "sc_0", "sc_1", "smx"], writes=["pp", "lsum"])

        if STOP <= 4:
            P.emit(); return nc
        n_ch = 0
        tot = 6 * NQ
        for ja in range(6):
            for tq in range(NQ):
                b = n_ch % NKB
                pvi = n_ch % 2
                n_ch += 1
                P.dma("pool", lambda g, b=b, ja=ja, tq=tq: g.indirect_dma_start(
                    out=vch[b][:], out_offset=None, in_=cv,
                    in_offset=bass.IndirectOffsetOnAxis(ap=idxs[:, ja, tq:tq + 1], axis=0)),
                    reads=["idxs"], writes=["kch%d" % b])
                c0 = ja * 128 + tq * TPC
                meng = "dve" if (n_ch % 5) < 3 else "pool"
                P.op(meng, lambda g, b=b, c0=c0, pvi=pvi: g.tensor_tensor(
                    out=prodv[pvi][:, :].rearrange("p (t d) -> p t d", d=HD), in0=vch[b][:, :].rearrange("p (t d) -> p t d", d=HD),
                    in1=pp[:, c0:c0 + TPC].unsqueeze(2).to_broadcast([NSA, TPC, HD]), op=ALU.mult),
                    reads=["kch%d" % b, "pp"], writes=["kcb%d" % pvi])
                for gi in range(NG):
                    first = (n_ch == 1 and gi == 0)
                    last = (n_ch == tot and gi == NG - 1)
                    P.op("pe", lambda pe, pvi=pvi, gi=gi, first=first, last=last: pe.matmul(
                        out=pb[4][:, :], lhsT=C.identb[:], rhs=prodv[pvi][:, gi * 512:(gi + 1) * 512], start=first, stop=last),
                        reads=["kcb%d" % pvi, "identb"], writes=[pbn[4]])
        P.op("dve", lambda v: v.tensor_reduce(out=oacc[:], in_=pb[4][:, :].rearrange("p (g d) -> p d g", d=HD), axis=AX.X, op=ALU.add),
             reads=[pbn[4]], writes=["oacc"])
        P.op("dve", lambda v: v.scalar_tensor_tensor(out=oacc[:], in0=vn[:], scalar=pp[:, 768:769], in1=oacc[:], op0=ALU.mult, op1=ALU.add),
             reads=["vn", "pp"], writes=["oacc"])
        P.op("dve", lambda v: v.reciprocal(out=lsum[:], in_=lsum[:]), reads=["lsum"], writes=["lsum"])
        P.op("dve", lambda v: v.scalar_tensor_tensor(out=oacc[:], in0=oacc[:], scalar=lsum[:, 0:1], in1=szs[:], op0=ALU.mult, op1=ALU.mult),
             reads=["lsum", "szs"], writes=["oacc"])
        P.dma("sp", lambda s: s.dma_start(out=gco, in_=oacc[:]), reads=["oacc"])
        P.emit()
    return nc


def build_B():
    nc = bass.Bass("TRN2", target_bir_lowering=False)

    def din(name, shape, dt=F32):
        return nc.dram_tensor(name, shape, dt, kind="ExternalInput").ap()

    def dout(name, shape, dt=F32):
        return nc.dram_tensor(name, shape, dt, kind="ExternalOutput").ap()

    xp = din("xp", [T, D])
    gsm = din("gsm", [NS, D])
    xs1m = din("xs1m", [NS, D])
    w_in0 = din("w_in0", [8, D, 512])
    wconv = din("wconv", [3, D])
    w_out0 = din("w_out0", [D, D])
    w_in1 = din("w_in1", [8, D, 512])
    w_out1 = din("w_out1", [D, D])
    lng = din("lng", [2, D])
    lnb = din("lnb", [2, D])
    cosp = din("cosp", [T, 16])
    sinp = din("sinp", [T, 16])

    yp = dout("yp", [T, D])
    ys = dout("ys", [NS, D])
    csp = dout("csp", [2, D])
    kp = dout("kp", [T, D])
    vp = dout("vp", [T, D])
    x1d = nc.dram_tensor("x1d", [T, D], F32, kind="Internal").ap()

    with contextlib.ExitStack() as st:
        def sb(name, shape, dt=F32):
            return st.enter_context(nc.sbuf_tensor(name, shape, dt))

        def ps(name, shape, dt=F32):
            return st.enter_context(nc.psum_tensor(name, shape, dt))

        P = Prog(nc)
        C = Common(nc, P, sb, ps, nb=4)
        pb = [ps("pb%d" % i, [128, 512], F32) for i in range(7)]
        pbn = ["pb%d" % i for i in range(7)]
        pbt = ps("pbt", [128, 1024], BF16)

        cst = C.x1t[3]
        onesb = sb("onesb", [128, 128], BF16)
        trib = sb("trib", [128, 128], BF16)
        z8 = sb("z8", [8, 8, 128], BF16)
        xT = sb("xT", [128, 8, T], BF16)
        gT = sb("gT", [128, 8, T], BF16)
        wcT = sb("wcT", [128, 8, 3], F32)
        scrA = sb("scrA", [128, T + 2], F32)
        scrB = sb("scrB", [128, 2048], F32)
        uT = scrA
        szT = scrA
        hsb, szb, gbb, cvb = (scrB[:, 0:512], scrB[:, 512:1024], scrB[:, 1024:1536], scrB[:, 1536:2048])
        qT = scrB[:, 0:1024].bitcast(BF16)
        kT = scrB[:, 1024:2048].bitcast(BF16)
        vb = sb("vb", [128, 16, 128], BF16)
        cosP = sb("cosP", [128, 16, 16], F32)
        sinP = sb("sinP", [128, 16, 16], F32)
        qkf = [sb("qkf%d" % i, [128, 2, 128], F32) for i in range(4)]
        qkb = [sb("qkb%d" % i, [128, 2, 128], BF16) for i in range(4)]
        vf = [sb("vf%d" % i, [128, 128], F32) for i in range(4)]
        rt = [sb("rt%d" % i, [128, 4, 2, 16], F32) for i in range(4)]
        kmT = sb("kmT", [128, 8], F32)
        kmTb = sb("kmTb", [128, 8], BF16)
        candb = sb("candb", [128, 8, 8], F32)
        gm = sb("gm", [128, 8, 8], F32)
        g2 = sb("g2", [128, 8, 8], F32)
        eqt = sb("eqt", [128, 8, 8], F32)
        mx1 = sb("mx1", [128, 8], F32)
        negb = sb("negb", [128, 16, 8], BF16)
        negbT = sb("negbT", [8, T], BF16)
        pT = [sb("pT%d" % i, [128, 2, 256], BF16) for i in range(3)]
        rl = sb("rl", [128, 256], F32)
        otmp = sb("otmp", [128, 256], F32)
        gsT = sb("gsT", [128, 8, NS], BF16)

        P.op("pool", lambda g: g.memset(onesb[:], 1.0), writes=["onesb"])
        P.op("pool", lambda g: g.memset(cst[:, 0:128], 0.0), writes=["x1t3"])
        P.op("pool", lambda g: g.affine_select(out=cst[:, 0:128], in_=cst[:, 0:128], pattern=[[1, 128]], compare_op=ALU.is_ge,
                                                fill=NEG, base=0, channel_multiplier=-1), writes=["x1t3"])
        P.op("pool", lambda g: g.tensor_copy(out=trib[:], in_=cst[:, 0:128]), reads=["x1t3"], writes=["trib"])
        z8v = cst[0:8, :].rearrange("p (n c) -> p n c", n=8)
        P.op("pool", lambda g: g.memset(cst[0:8, :], 1.0), writes=["x1t3"])
        P.op("pool", lambda g: g.affine_select(out=z8v, in_=z8v, pattern=[[1, 8], [0, 128]], compare_op=ALU.is_equal,
                                                fill=0.0, base=0, channel_multiplier=-1), writes=["x1t3"])
        P.op("pool", lambda g: g.tensor_copy(out=z8[:], in_=z8v), reads=["x1t3"], writes=["z8"])
        P.op("pool", lambda g: g.memset(candb[:], 0.0), writes=["candb"])
        for j in range(8):
            qb = 4 + j // 2
            P.op("pool", lambda g, j=j, qb=qb: g.affine_select(out=candb[:, j, :], in_=candb[:, j, :], pattern=[[-1, 8]],
                                                               compare_op=ALU.is_ge, fill=-BIG, base=qb - 1, channel_multiplier=0),
                 writes=["candb"])
        nbs = cst[:, 0:64].rearrange("p (t n) -> p t n", t=8)
        P.op("pool", lambda g: g.memset(cst[:, 0:64], 0.0), reads=["x1t3"], writes=["x1t3"])
        for j in range(8):
            qb = j // 2
            P.op("pool", lambda g, j=j, qb=qb: g.affine_select(out=nbs[:, j, :], in_=nbs[:, j, :], pattern=[[-1, 8]],
                                                               compare_op=ALU.is_ge, fill=NEG, base=qb - 1, channel_multiplier=0),
                 writes=["x1t3"])
        P.op("pool", lambda g: g.tensor_copy(out=negb[:, 0:8, :], in_=nbs), reads=["x1t3"], writes=["negb_s"])
        P.op("pool", lambda g: g.memset(uT[:, 0:2], 0.0), writes=["scrA"])

        for j in range(3):
            P.dma("sp", lambda s, j=j: s.dma_start(out=wcT[:, :, j], in_=wconv[j, :].rearrange("(fc p) -> p fc", p=128),
                                                   allow_slow_non_contiguous=True), writes=["wcT"])
        P.dma("sp", lambda s: s.dma_start(out=cosP[:], in_=cosp.rearrange("(t p) i -> p t i", p=128)), writes=["cosP"])
        P.dma("sp", lambda s: s.dma_start(out=sinP[:], in_=sinp.rearrange("(t p) i -> p t i", p=128)), writes=["sinP"])

        def a1_tiles():
            for tt in range(16):
                i = tt % 3
                P.dma("sp", lambda s, tt=tt, i=i: s.dma_start(out=C.xt_ring[i][:], in_=xp[tt * 128:(tt + 1) * 128, :]), writes=["xt%d" % i])
                C.transpose_to(C.xt_ring[i], 128, xT, tt * 128, "xt%d" % i, "xT_%d" % tt, [pb[4], pb[5]], pbn[4:6])
                yield

        a1 = a1_tiles()
        a1_done = [0]

        def a1_pump(upto):
            while a1_done[0] < min(upto, 16):
                next(a1)
                a1_done[0] += 1

        a1_pump(4)
        for fc in range(8):
            wi = C.load_w_block(wblock_view(w_in0, fc))
            for tt in range(4):
                a1_pump(4 * tt + 8)
                cols = slice(tt * 512, (tt + 1) * 512)
                for g, bi in ((2, 2), (3, 3), (1, 1), (0, 0)):
                    for kc in range(8):
                        P.op("pe", lambda pe, g=g, bi=bi, kc=kc, wi=wi, cols=cols: pe.matmul(
                            out=pb[bi][:, :], lhsT=C.wbb[wi][:, kc, g * 128:(g + 1) * 128], rhs=xT[:, kc, cols],
                            start=(kc == 0), stop=(kc == 7)),
                            reads=["wbb%d" % wi] + ["xT_%d" % t2 for t2 in range(tt * 4, tt * 4 + 4)], writes=[pbn[bi]])
                P.op("act", lambda a: a.activation(out=hsb, in_=pb[2][:, :], func=AF.Copy), reads=[pbn[2]], writes=["hsb"])
                P.op("act", lambda a: a.activation(out=szb, in_=pb[3][:, :], func=AF.Silu), reads=[pbn[3]], writes=["szb"])
                uo = 2 + tt * 512
                P.op("dve", lambda v, uo=uo: v.tensor_tensor(out=uT[:, uo:uo + 512], in0=pb[1][:, :], in1=hsb, op=ALU.mult),
                     reads=[pbn[1], "hsb"], writes=["scrA"])
                P.op("dve", lambda v, uo=uo, fc=fc: v.tensor_scalar(out=cvb, in0=uT[:, uo:uo + 512], scalar1=wcT[:, fc, 2:3], scalar2=None,
                                                                     op0=ALU.mult), reads=["scrA", "wcT"], writes=["cvb"])
                P.op("dve", lambda v, uo=uo, fc=fc: v.scalar_tensor_tensor(out=cvb, in0=uT[:, uo - 1:uo + 511], scalar=wcT[:, fc, 1:2], in1=cvb,
                                                                            op0=ALU.mult, op1=ALU.add), reads=["scrA"], writes=["cvb"])
                P.op("dve", lambda v, uo=uo, fc=fc: v.scalar_tensor_tensor(out=cvb, in0=uT[:, uo - 2:uo + 510], scalar=wcT[:, fc, 0:1], in1=cvb,
                                                                            op0=ALU.mult, op1=ALU.add), reads=["scrA"], writes=["cvb"])
                P.op("dve", lambda v: v.tensor_tensor(out=gbb, in0=pb[0][:, :], in1=szb, op=ALU.mult), reads=[pbn[0], "szb"], writes=["gbb"])
                P.op("dve", lambda v, fc=fc, cols=cols: v.tensor_tensor(out=gT[:, fc, cols], in0=gbb, in1=cvb, op=ALU.mult),
                     reads=["gbb", "cvb"], writes=["gT"])
            P.dma("sp", lambda s, fc=fc: s.dma_start(out=csp.rearrange("j (f p) -> p f j", p=128)[:, fc, :], in_=uT[:, T:T + 2],
                                                      allow_slow_non_contiguous=True), reads=["scrA"])

        C.load_wout(w_out0)
        C.load_ln(lng[0:1, :], lnb[0:1, :])
        def xres0(tt):
            i = (base0 + tt) % C.NB
            P.dma("sp", lambda s: s.dma_start(out=C.xt_ring[i][:], in_=xp[tt * 128:(tt + 1) * 128, :]), writes=["xt%d" % i])

        base0 = C.tctr
        xres0(0)
        xres0(1)
        pendq = []
        for tt in range(16):
            def outf(i, tt=tt):
                P.dma("sp", lambda s: s.dma_start(out=x1d[tt * 128:(tt + 1) * 128, :], in_=C.x1t[i][:]),
                      reads=["x1t%d" % i], writes=["x1d_%d" % tt])
                C.transpose_to(C.x1t[i], 128, xT, tt * 128, "x1t%d" % i, "xT_%d" % tt, [pb[4], pb[5]], pbn[4:6])
            bsel = (tt % 2) * 2
            nxt = C.outproj_ln(128, lambda fc, tt=tt: gT[:, fc, tt * 128:(tt + 1) * 128], "gT", None, outf,
                               [pb[bsel], pb[bsel + 1]], pbn[bsel:bsel + 2], defer=True)
            if tt + 2 < 16:
                xres0(tt + 2)
            pendq.append(nxt)
            if len(pendq) > 3:
                pendq.pop(0)()
        for f_ in pendq:
            f_()

        pS = [pb[0], pb[1]]
        pSn = pbn[0:2]
        pO, pL, pOn, pLn = pb[2], pb[3], pbn[2], pbn[3]
        pQKV, pQn = [pb[4], pb[5]], pbn[4:6]
        pM, pMn = pb[6], pbn[6]
        szT2 = sb("szT2", [128, T], F32)
        qk2 = sb("qk2", [128, 2048], F32)
        vb2 = sb("vb2", [128, 16, 128], BF16)
        negb2 = sb("negb2", [128, 16, 8], BF16)
        negbT2 = sb("negbT2", [8, T], BF16)
        P.op("pool", lambda g: g.tensor_copy(out=negb2[:, 0:8, :], in_=nbs), reads=["x1t3"], writes=["negb_s1"])
        szTs = [szT, szT2]
        szTn = ["scrA", "szT2"]
        qTs = [qT, qk2[:, 0:1024].bitcast(BF16)]
        kTs = [kT, qk2[:, 1024:2048].bitcast(BF16)]
        vbs = [vb, vb2]
        negbs = [negb, negb2]
        negbTs = [negbT, negbT2]

        def proj_steps(h):
            par = h % 2
            szT_, qT_, kT_, vb_, negb_, negbT_ = szTs[par], qTs[par], kTs[par], vbs[par], negbs[par], negbTs[par]
            rq, rk, rv, rz, rnb, rnbT = "qT%d" % par, "kT%d" % par, "vb%d" % par, szTn[par], "negb_d%d" % par, "negbT%d" % par
            wi = C.load_w_block(wblock_view(w_in1, h))
            for tt in range(4):
                bank, bn = pQKV[tt % 2], pQn[tt % 2]
                cols = slice(tt * 512, (tt + 1) * 512)
                for kc in range(8):
                    P.op("pe", lambda pe, kc=kc, bank=bank, cols=cols, wi=wi: pe.matmul(
                        out=bank[:, :], lhsT=C.wbb[wi][:, kc, 384:512], rhs=xT[:, kc, cols], start=(kc == 0), stop=(kc == 7)),
                        reads=["wbb%d" % wi] + ["xT_%d" % t2 for t2 in range(tt * 4, tt * 4 + 4)], writes=[bn])
                P.op("act", lambda a, bank=bank, cols=cols: a.activation(out=szT_[:, cols], in_=bank[:, :], func=AF.Silu),
                     reads=[bn], writes=[rz])
                yield

            def emit_tr(tt):
                j = tt % 4
                t4 = tt % 4
                for a in range(2):
                    P.op("pe", lambda pe, a=a, j=j, t4=t4: pe.transpose(
                        out=pbt[:, a * 512 + t4 * 128:a * 512 + (t4 + 1) * 128], in_=qkb[j][:, a, :], identity=C.identb[:]),
                        reads=["qkb%d" % j, "identb"], writes=["pbt"])
                if t4 == 3:
                    c0 = (tt - 3) * 128
                    P.op("act", lambda a, c0=c0: a.activation(out=qT_[:, c0:c0 + 512], in_=pbt[:, 0:512], func=AF.Copy),
                         reads=["pbt"], writes=[rq])
                    P.op("act", lambda a, c0=c0: a.activation(out=kT_[:, c0:c0 + 512], in_=pbt[:, 512:1024], func=AF.Copy),
                         reads=["pbt"], writes=[rk])

            for tt in range(16):
                bank, bn = pQKV[tt % 2], pQn[tt % 2]
                j = tt % 4
                for kc in range(8):
                    P.op("pe", lambda pe, kc=kc, bank=bank, tt=tt, wi=wi: pe.matmul(
                        out=bank[:, 0:384], lhsT=xT[:, kc, tt * 128:(tt + 1) * 128], rhs=C.wbb[wi][:, kc, 0:384],
                        start=(kc == 0), stop=(kc == 7)), reads=["wbb%d" % wi, "xT_%d" % tt], writes=[bn])
                qk_ps = bank[:, 0:256].rearrange("p (a c) -> p a c", a=2)
                P.op("act", lambda a, qk_ps=qk_ps, j=j: a.activation(out=qkf[j][:], in_=qk_ps, func=AF.Copy),
                     reads=[bn], writes=["qkf%d_hi" % j, "qkf%d_lo" % j, "qkf%d_lo2" % j])
                P.op("act", lambda a, bank=bank, j=j: a.activation(out=vf[j][:], in_=bank[:, 256:384], func=AF.Copy),
                     reads=[bn], writes=["vf%d" % j])
                P.op("pool", lambda g, j=j, tt=tt: g.tensor_copy(out=vb_[:, tt, :], in_=vf[j][:]),
                     reads=["vf%d" % j], writes=[rv])
                cs = cosP[:, tt, :].unsqueeze(1).to_broadcast([128, 2, 16])
                sn = sinP[:, tt, :].unsqueeze(1).to_broadcast([128, 2, 16])
                x1v, x2v = qkf[j][:, :, 0:16], qkf[j][:, :, 16:32]
                rtr = "rt%d" % j
                qlo = "qkf%d_lo" % j
                P.op("dve", lambda v, j=j, x1v=x1v, cs=cs: v.tensor_tensor(out=rt[j][:, 0], in0=x1v, in1=cs, op=ALU.mult),
                     reads=[qlo, "cosP"], writes=[rtr + "a"])
                P.op("dve", lambda v, j=j, x2v=x2v, sn=sn: v.tensor_tensor(out=rt[j][:, 1], in0=x2v, in1=sn, op=ALU.mult),
                     reads=[qlo, "sinP"], writes=[rtr + "b"])
                P.op("dve", lambda v, j=j, x2v=x2v, cs=cs: v.tensor_tensor(out=rt[j][:, 2], in0=x2v, in1=cs, op=ALU.mult),
                     reads=[qlo], writes=[rtr + "c"])
                P.op("dve", lambda v, j=j, x1v=x1v, sn=sn: v.tensor_tensor(out=rt[j][:, 3], in0=x1v, in1=sn, op=ALU.mult),
                     reads=[qlo], writes=[rtr + "d"])
                P.op("dve", lambda v, j=j: v.tensor_tensor(out=qkf[j][:, :, 0:16], in0=rt[j][:, 0], in1=rt[j][:, 1], op=ALU.subtract),
                     reads=[rtr + "a", rtr + "b", rtr + "c", rtr + "d"], writes=["qkf%d_lo" % j])
                P.op("dve", lambda v, j=j: v.tensor_tensor(out=qkf[j][:, :, 16:32], in0=rt[j][:, 2], in1=rt[j][:, 3], op=ALU.add),
                     reads=[rtr + "c", rtr + "d"], writes=["qkf%d_lo2" % j])
                P.dma("sp", lambda s, j=j, tt=tt, h=h: s.dma_start(out=kp[tt * 128:(tt + 1) * 128, h * 128:(h + 1) * 128], in_=qkf[j][:, 1, :]),
                      reads=["qkf%d_lo" % j, "qkf%d_lo2" % j, "qkf%d_hi" % j])
                P.dma("sp", lambda s, j=j, tt=tt, h=h: s.dma_start(out=vp[tt * 128:(tt + 1) * 128, h * 128:(h + 1) * 128], in_=vf[j][:]),
                      reads=["vf%d" % j])
                P.op("pool", lambda g, j=j: g.tensor_copy(out=qkb[j][:], in_=qkf[j][:]),
                     reads=["qkf%d_lo" % j, "qkf%d_lo2" % j, "qkf%d_hi" % j], writes=["qkb%d" % j])
                if tt >= 2:
                    emit_tr(tt - 2)
                yield
            emit_tr(14)
            emit_tr(15)
            yield
            P.op("dve", lambda v: v.tensor_reduce(out=kmT[:], in_=kT_.rearrange("p (n t) -> p n t", n=8), axis=AX.X, op=ALU.add),
                 reads=[rk], writes=["kmT"])
            P.op("dve", lambda v: v.tensor_scalar(out=kmTb[:], in0=kmT[:], scalar1=1.0 / 256.0, scalar2=None, op0=ALU.mult),
                 reads=["kmT"], writes=["kmTb"])
            for j in range(8):
                tt = 8 + j
                P.op("pe", lambda pe, j=j, tt=tt: pe.matmul(out=pM[:, j * 8:(j + 1) * 8], lhsT=qT_[:, tt * 128:(tt + 1) * 128], rhs=kmTb[:],
                                                            start=True, stop=True), reads=[rq, "kmTb"], writes=[pMn])
            P.op("dve", lambda v: v.tensor_tensor(out=gm[:], in0=pM[:, 0:64].rearrange("p (a b) -> p a b", a=8), in1=candb[:], op=ALU.add),
                 reads=[pMn, "candb"], writes=["gm"])
            yield

            def bc8(t):
                return t[:, :].unsqueeze(2).to_broadcast([128, 8, 8])
            P.op("dve", lambda v: v.tensor_reduce(out=mx1[:], in_=gm[:], axis=AX.X, op=ALU.max), reads=["gm"], writes=["mx1"])
            P.op("dve", lambda v: v.tensor_tensor(out=eqt[:], in0=gm[:], in1=bc8(mx1), op=ALU.is_ge), reads=["gm", "mx1"], writes=["eqt"])
            P.op("dve", lambda v: v.scalar_tensor_tensor(out=g2[:], in0=eqt[:], scalar=-BIG, in1=gm[:], op0=ALU.mult, op1=ALU.add),
                 reads=["eqt", "gm"], writes=["g2"])
            P.op("dve", lambda v: v.tensor_reduce(out=mx1[:], in_=g2[:], axis=AX.X, op=ALU.max), reads=["g2"], writes=["mx1"])
            P.op("dve", lambda v: v.tensor_tensor(out=eqt[:], in0=g2[:], in1=bc8(mx1), op=ALU.is_ge), reads=["g2", "mx1"], writes=["eqt"])
            yield
            P.op("dve", lambda v: v.scalar_tensor_tensor(out=g2[:], in0=eqt[:], scalar=-BIG, in1=g2[:], op0=ALU.mult, op1=ALU.add),
                 reads=["eqt"], writes=["g2"])
            P.op("dve", lambda v: v.tensor_reduce(out=mx1[:], in_=g2[:], axis=AX.X, op=ALU.max), reads=["g2"], writes=["mx1"])
            P.op("dve", lambda v: v.tensor_tensor(out=eqt[:], in0=gm[:], in1=bc8(mx1), op=ALU.is_ge), reads=["gm", "mx1"], writes=["eqt"])
            P.op("dve", lambda v: v.tensor_scalar(out=negb_[:, 8:16, :], in0=eqt[:], scalar1=-NEG, scalar2=NEG, op0=ALU.mult, op1=ALU.add),
                 reads=["eqt"], writes=[rnb])
            yield
            for hf in range(2):
                for t8 in range(8):
                    tt = hf * 8 + t8
                    P.op("pe", lambda pe, tt=tt, t8=t8: pe.transpose(out=pbt[0:8, t8 * 128:(t8 + 1) * 128], in_=negb_[:, tt, :],
                                                                     identity=C.identb[:]),
                         reads=["negb_s", "negb_s1", rnb, "identb"], writes=["pbt"])
                P.op("act", lambda a, hf=hf: a.activation(out=negbT_[:, hf * 1024:(hf + 1) * 1024], in_=pbt[0:8, :], func=AF.Copy),
                     reads=["pbt"], writes=[rnbT])
                yield

        def attn_steps(h):
            par = h % 2
            szT_, qT_, kT_, vb_, negbT_ = szTs[par], qTs[par], kTs[par], vbs[par], negbTs[par]
            rq, rk, rv, rz, rnbT = "qT%d" % par, "kT%d" % par, "vb%d" % par, szTn[par], "negbT%d" % par
            rounds = [(qb, n) for qb in range(8) for n in range(qb + 1)]

            def emit_qk(r):
                qb, n = rounds[r]
                bank, bn = pS[r % 2], pSn[r % 2]
                qc = qb * 256
                sv = bank[:, :].rearrange("p (a q) -> p a q", a=2)
                for a in range(2):
                    kt = 2 * n + a
                    lk = kT_[:, kt * 128:(kt + 1) * 128]
                    if n < qb:
                        P.op("pe", lambda pe, a=a, lk=lk, qc=qc, sv=sv: pe.matmul(out=sv[:, a, :], lhsT=lk, rhs=qT_[:, qc:qc + 256],
                                                                                 start=True, stop=False),
                             reads=[rk, rq], writes=[bn])
                        P.op("pe", lambda pe, a=a, n=n, qc=qc, sv=sv: pe.matmul(out=sv[:, a, :], lhsT=z8[:, n, :], rhs=negbT_[:, qc:qc + 256],
                                                                                start=False, stop=True),
                             reads=["z8", rnbT], writes=[bn])
                    else:
                        if a == 0:
                            P.op("pe", lambda pe, lk=lk, qc=qc, sv=sv: pe.matmul(out=sv[:, 0, 0:128], lhsT=lk, rhs=qT_[:, qc:qc + 128],
                                                                                 start=True, stop=False), reads=[rk, rq], writes=[bn])
                            P.op("pe", lambda pe, sv=sv: pe.matmul(out=sv[:, 0, 0:128], lhsT=C.identb[:], rhs=trib[:], start=False, stop=True),
                                 reads=["identb", "trib"], writes=[bn])
                            P.op("pe", lambda pe, lk=lk, qc=qc, sv=sv: pe.matmul(out=sv[:, 0, 128:256], lhsT=lk, rhs=qT_[:, qc + 128:qc + 256],
                                                                                 start=True, stop=True), reads=[rk, rq], writes=[bn])
                        else:
                            P.op("pe", lambda pe, lk=lk, qc=qc, sv=sv: pe.matmul(out=sv[:, 1, 128:256], lhsT=lk, rhs=qT_[:, qc + 128:qc + 256],
                                                                                 start=True, stop=False), reads=[rk, rq], writes=[bn])
                            P.op("pe", lambda pe, sv=sv: pe.matmul(out=sv[:, 1, 128:256], lhsT=C.identb[:], rhs=trib[:], start=False, stop=True),
                                 reads=["identb", "trib"], writes=[bn])

            def emit_exp(r):
                qb, n = rounds[r]
                bank, bn = pS[r % 2], pSn[r % 2]
                pt_, ptn = pT[r % 3], "pT%d" % (r % 3)
                sv = bank[:, :].rearrange("p (a q) -> p a q", a=2)
                if n < qb:
                    P.op("act", lambda a: a.activation(out=pt_[:], in_=sv, func=AF.Exp, scale=SCALE), reads=[bn], writes=[ptn])
                else:
                    P.op("act", lambda a: a.activation(out=pt_[:, 0, :], in_=sv[:, 0, :], func=AF.Exp, scale=SCALE), reads=[bn], writes=[ptn])
                    P.op("act", lambda a: a.activation(out=pt_[:, 1, 128:256], in_=sv[:, 1, 128:256], func=AF.Exp, scale=SCALE),
                         reads=[bn], writes=[ptn])

            def emit_pv(r):
                qb, n = rounds[r]
                pt_, ptn = pT[r % 3], "pT%d" % (r % 3)
                for a in range(2):
                    kt = 2 * n + a
                    own1 = (n == qb and a == 1)
                    q0 = 128 if own1 else 0
                    first = (n == 0 and a == 0)
                    last = (n == qb and a == 1)
                    P.op("pe", lambda pe, a=a, kt=kt, q0=q0, first=first, last=last: pe.matmul(
                        out=pO[:, q0:256], lhsT=vb_[:, kt, :], rhs=pt_[:, a, q0:256], start=first, stop=last),
                        reads=[rv, ptn], writes=[pOn])
                    P.op("pe", lambda pe, a=a, q0=q0, first=first, last=last: pe.matmul(
                        out=pL[:, q0:256], lhsT=onesb[:], rhs=pt_[:, a, q0:256], start=first, stop=last),
                        reads=["onesb", ptn], writes=[pLn])
                if n == qb:
                    qc = qb * 256
                    P.op("dve", lambda v: v.reciprocal(out=rl[:], in_=pL[:, 0:256]), reads=[pLn], writes=["rl"])
                    P.op("dve", lambda v: v.tensor_tensor(out=otmp[:], in0=pO[:, 0:256], in1=rl[:], op=ALU.mult),
                         reads=[pOn, "rl"], writes=["otmp"])
                    P.op("dve", lambda v, qc=qc: v.tensor_tensor(out=gT[:, h, qc:qc + 256], in0=otmp[:], in1=szT_[:, qc:qc + 256], op=ALU.mult),
                         reads=["otmp", rz], writes=["gT"])

            nr = len(rounds)
            emit_qk(0)
            emit_exp(0)
            for r in range(nr):
                if r + 1 < nr:
                    emit_qk(r + 1)
                    emit_exp(r + 1)
                emit_pv(r)
                yield

        for _ in proj_steps(0):
            pass
        for h in range(NH):
            ag = attn_steps(h)
            pg_ = proj_steps(h + 1) if h + 1 < NH else iter(())
            pdone = False
            k = 0
            for _ in ag:
                k += 1
                if not pdone and k % 4 != 0:
                    try:
                        next(pg_)
                    except StopIteration:
                        pdone = True
            for _ in pg_:
                pass

        C.load_wout(w_out1)
        C.load_ln(lng[1:2, :], lnb[1:2, :])
        def xres1(tt):
            i = (base1 + tt) % C.NB
            P.dma("sp", lambda s: s.dma_start(out=C.xt_ring[i][:], in_=x1d[tt * 128:(tt + 1) * 128, :]),
                  reads=["x1d_%d" % tt], writes=["xt%d" % i])

        base1 = C.tctr
        xres1(0)
        xres1(1)
        for tt in range(16):
            def outf(i, tt=tt):
                P.dma("sp", lambda s: s.dma_start(out=yp[tt * 128:(tt + 1) * 128, :], in_=C.x1t[i][:]), reads=["x1t%d" % i])
            bsel = 2 + (tt % 2) * 2
            nxt = C.outproj_ln(128, lambda fc, tt=tt: gT[:, fc, tt * 128:(tt + 1) * 128], "gT", None, outf,
                               [pb[bsel], pb[bsel + 1]], pbn[bsel:bsel + 2], defer=True)
            if tt + 2 < 16:
                xres1(tt + 2)
            nxt()
        P.dma("sp", lambda s: s.dma_start(out=cst[0:NS, :], in_=gsm), reads=["x1t3"], writes=["x1t3"])
        C.transpose_to(cst, NS, gsT, 0, "x1t3", "gsT", [pb[0], pb[1]], pbn[0:2])

        def xres_s(i):
            P.dma("sp", lambda s: s.dma_start(out=C.xt_ring[i][0:NS, :], in_=xs1m), writes=["xt%d" % i])

        def outf_s(i):
            P.dma("sp", lambda s: s.dma_start(out=ys, in_=C.x1t[i][0:NS, :]), reads=["x1t%d" % i])
        C.outproj_ln(NS, lambda fc: gsT[:, fc, :], "gsT", xres_s, outf_s, [pb[4], pb[5]], pbn[4:6])
        P.emit()
    return nc


def _rope_tables(pos):
    half = 16 // 1
    inv = (np.float32(500000.0) ** (-np.arange(16, dtype=np.float32) * np.float32(2.0) / np.float32(32.0))).astype(np.float32)
    ang = pos.astype(np.float32)[:, None] * inv[None, :]
    return np.cos(ang).astype(np.float32), np.sin(ang).astype(np.float32)


def kernel(x_prompt, x_sample, state_conv, cache_k, cache_v, page_table,
           w_in_conv, w_conv, w_out_conv, w_in_attn, w_out_attn, ln_g, ln_b):
    f = lambda a: np.ascontiguousarray(np.asarray(a, dtype=np.float32))
    x_prompt, x_sample, state_conv = f(x_prompt), f(x_sample), f(state_conv)
    w_in_conv, w_conv, w_out_conv = f(w_in_conv), f(w_conv), f(w_out_conv)
    w_in_attn, w_out_attn, ln_g, ln_b = f(w_in_attn), f(w_out_attn), f(ln_g), f(ln_b)
    cache_k = np.asarray(cache_k)
    cache_v = np.asarray(cache_v)
    pt = np.ascontiguousarray(np.asarray(page_table, dtype=np.int32))
    w_in0_blk = _block_w(w_in_conv[0])
    w_in1_blk = _block_w(w_in_attn[0])
    cos_p, sin_p = _rope_tables(np.arange(T))
    cos_s, sin_s = _rope_tables(np.array([2048]))

    xs = np.ascontiguousarray(x_sample[:, 0, :])
    stc = np.ascontiguousarray(state_conv[0])
    in_a = []
    for c in range(8):
        ckc = np.ascontiguousarray(cache_k[0, :, :, c, :], dtype=np.float32).reshape(NPHYS * NQ, CH)
        cvc = np.ascontiguousarray(cache_v[0, :, :, c, :], dtype=np.float32).reshape(NPHYS * NQ, CH)
        w1c = w_in1_blk[c]
        in_a.append({"xs": xs, "stc": stc, "ptab": pt, "ck": ckc, "cv": cvc, "w_in0": w_in0_blk, "wconv": w_conv[0],
                     "w_out0": w_out_conv[0], "w1c": w1c, "lng0": ln_g[0:1], "lnb0": ln_b[0:1], "coss": cos_s, "sins": sin_s})
    ra = run_bass_kernel_spmd(build_A(), in_a, core_ids=list(range(8))).results
    conv_state_sample = ra[0]["css"][None]
    xs1 = ra[0]["xs1o"]
    k_sample = np.stack([ra[c]["ksm"] for c in range(8)], axis=1)[None, :, None]
    v_sample = np.stack([ra[c]["vsm"] for c in range(8)], axis=1)[None, :, None]
    g_all = np.concatenate([ra[c]["gco"] for c in range(8)], axis=1)

    in_b = []
    for c in range(8):
        in_b.append({"xp": x_prompt[c], "gsm": np.ascontiguousarray(g_all[16 * c:16 * c + 16]),
                     "xs1m": np.ascontiguousarray(xs1[16 * c:16 * c + 16]),
                     "w_in0": w_in0_blk, "wconv": w_conv[0], "w_out0": w_out_conv[0], "w_in1": w_in1_blk,
                     "w_out1": w_out_attn[0], "lng": ln_g, "lnb": ln_b, "cosp": cos_p, "sinp": sin_p})
    rb = run_bass_kernel_spmd(build_B(), in_b, core_ids=list(range(8))).results
    y_prompt = np.stack([rb[c]["yp"] for c in range(8)], axis=0)
    y_sample = np.concatenate([rb[c]["ys"] for c in range(8)], axis=0)[:, None, :]
    conv_state_prompt = np.stack([rb[c]["csp"] for c in range(8)], axis=0)[None]
    k_prompt = np.stack([rb[c]["kp"] for c in range(8)], axis=0).reshape(1, 8, 16, 128, 8, 128)
    v_prompt = np.stack([rb[c]["vp"] for c in range(8)], axis=0).reshape(1, 8, 16, 128, 8, 128)
    return (y_prompt.astype(np.float32), y_sample.astype(np.float32), conv_state_prompt.astype(np.float32),
            conv_state_sample.astype(np.float32), k_prompt.astype(np.float32), v_prompt.astype(np.float32),
            k_sample.astype(np.float32), v_sample.astype(np.float32))
```
